# Optimizing a Trainium2 kernel written in Bass

```python
import math
import jax, jax.numpy as jnp
from jax import lax
import numpy as np

D_MODEL = 1024
BATCH = 8
SEQ = 8192
DEPTH = 2

N_MIXERS = 2
D_FF = 2816
FFN_RES = 0.5
POOL_WINDOWS = (2, 4, 8, 16)
POOL_GROUPS = len(POOL_WINDOWS)
POOL_GW = D_MODEL // POOL_GROUPS
N_HEADS = 8
HEAD_DIM = D_MODEL // (2 * N_HEADS)
V_DIM = 2 * HEAD_DIM
QK_W = N_HEADS * HEAD_DIM
QKV_W = 4 * QK_W + N_HEADS * V_DIM
ATTN_SCALE = HEAD_DIM ** -0.5
Q_BLOCK = 128
LN_EPS = 1e-5
RMS_EPS = 1e-5
DEEPNORM_ALPHA = (2.0 * DEPTH) ** 0.25
DEEPNORM_BETA = (8.0 * DEPTH) ** -0.25

kernel_name = "hybrid_pool_diffattn_macaron_deepnorm"


def layer_norm(x, g, b):
    xf = x.astype(jnp.float32)
    mu = jnp.mean(xf, axis=-1, keepdims=True)
    var = jnp.mean(jnp.square(xf - mu), axis=-1, keepdims=True)
    y = (xf - mu) * lax.rsqrt(var + LN_EPS) * g.astype(jnp.float32) + b.astype(jnp.float32)
    return y.astype(x.dtype)


def deepnorm(x, y, g, b):
    return layer_norm(DEEPNORM_ALPHA * x + y, g, b)


def swiglu(x, w_in, w_out):
    gate, up = jnp.split(x @ w_in, 2, axis=-1)
    return (jax.nn.silu(gate) * up) @ w_out


def pool_mixer(x, w_pool, scale):
    B, S, D = x.shape
    xf = x.astype(jnp.float32)
    c = jnp.cumsum(xf, axis=1)
    pos = jnp.arange(S)
    outs = []
    for g, w in enumerate(POOL_WINDOWS):
        sl = slice(g * POOL_GW, (g + 1) * POOL_GW)
        cg = c[..., sl]
        shifted = jnp.pad(cg[:, :S - w], ((0, 0), (w, 0), (0, 0)))
        count = jnp.minimum(pos + 1, w).astype(jnp.float32)[None, :, None]
        d = ((cg - shifted) / count - xf[..., sl]).astype(x.dtype)
        outs.append(d @ w_pool[g])
    return jnp.concatenate(outs, axis=-1) * scale


def diff_attention(x, w_qkv, lam_q1, lam_k1, lam_q2, lam_k2, subln_g, w_o, lambda_init):
    B, S, _ = x.shape
    qkv = x @ w_qkv
    q1, q2, k1, k2, v = jnp.split(qkv, [QK_W, 2 * QK_W, 3 * QK_W, 4 * QK_W], axis=-1)

    def heads(t, d):
        return t.reshape(B, S, N_HEADS, d).transpose(0, 2, 1, 3)

    q1, q2, k1, k2 = (heads(t, HEAD_DIM) for t in (q1, q2, k1, k2))
    vf = heads(v, V_DIM).astype(jnp.float32)

    lam = (jnp.exp(jnp.sum(lam_q1.astype(jnp.float32) * lam_k1.astype(jnp.float32)))
           - jnp.exp(jnp.sum(lam_q2.astype(jnp.float32) * lam_k2.astype(jnp.float32)))
           + lambda_init)

    slopes = jnp.exp2(-8.0 * jnp.arange(1, N_HEADS + 1, dtype=jnp.float32) / N_HEADS)
    n_blocks = S // Q_BLOCK
    k_pos = jnp.arange(S)

    def to_blocks(t):
        return t.reshape(B, N_HEADS, n_blocks, Q_BLOCK, HEAD_DIM).transpose(2, 0, 1, 3, 4)

    def one_block(args):
        q1b, q2b, blk = args
        q_pos = blk * Q_BLOCK + jnp.arange(Q_BLOCK)
        dist = (q_pos[:, None] - k_pos[None, :]).astype(jnp.float32)
        causal = dist >= 0
        bias = -slopes[:, None, None] * dist

        def probs(qb, k):
            s = jnp.einsum('bhqd,bhkd->bhqk', qb, k).astype(jnp.float32) * ATTN_SCALE + bias
            return jax.nn.softmax(jnp.where(causal, s, -jnp.inf), axis=-1)

        a = probs(q1b, k1) - lam * probs(q2b, k2)
        return jnp.einsum('bhqk,bhke->bhqe', a, vf)

    o = lax.map(one_block, (to_blocks(q1), to_blocks(q2), jnp.arange(n_blocks)))
    o = o.transpose(1, 0, 3, 2, 4).reshape(B, S, N_HEADS, V_DIM)
    o = o * lax.rsqrt(jnp.mean(jnp.square(o), axis=-1, keepdims=True) + RMS_EPS)
    o = o * subln_g.astype(jnp.float32) * (1.0 - lambda_init)
    return o.reshape(B, S, N_HEADS * V_DIM).astype(x.dtype) @ w_o


def setup_inputs(seed: int = 0) -> dict:
    key = jax.random.key(seed)
    ks = iter(jax.random.split(key, 64))
    f32 = jnp.float32

    def nrm(shape, scale):
        return jax.random.normal(next(ks), shape, f32) * scale

    def gain(n):
        return 1.0 + 0.02 * jax.random.normal(next(ks), (n,), f32)

    def bias(n):
        return 0.02 * jax.random.normal(next(ks), (n,), f32)

    def ffn():
        return (nrm((D_MODEL, 2 * D_FF), D_MODEL ** -0.5),
                nrm((D_FF, D_MODEL), D_FF ** -0.5 * DEEPNORM_BETA))

    inp = {"x": jax.random.normal(next(ks), (BATCH, SEQ, D_MODEL), f32)}
    inp["l0_ffn1_w_in"], inp["l0_ffn1_w_out"] = ffn()
    inp["l0_ln1_g"], inp["l0_ln1_b"] = gain(D_MODEL), bias(D_MODEL)
    inp["l0_pool_w"] = nrm((POOL_GROUPS, POOL_GW, POOL_GW), POOL_GW ** -0.5 * DEEPNORM_BETA)
    inp["l0_pool_scale"] = gain(D_MODEL)
    inp["l0_ln2_g"], inp["l0_ln2_b"] = gain(D_MODEL), bias(D_MODEL)
    inp["l0_ffn2_w_in"], inp["l0_ffn2_w_out"] = ffn()
    inp["l0_ln3_g"], inp["l0_ln3_b"] = gain(D_MODEL), bias(D_MODEL)
    inp["l1_ffn1_w_in"], inp["l1_ffn1_w_out"] = ffn()
    inp["l1_ln1_g"], inp["l1_ln1_b"] = gain(D_MODEL), bias(D_MODEL)
    inp["l1_w_qkv"] = nrm((D_MODEL, QKV_W), D_MODEL ** -0.5)
    inp["l1_lam_q1"] = nrm((HEAD_DIM,), 0.1)
    inp["l1_lam_k1"] = nrm((HEAD_DIM,), 0.1)
    inp["l1_lam_q2"] = nrm((HEAD_DIM,), 0.1)
    inp["l1_lam_k2"] = nrm((HEAD_DIM,), 0.1)
    inp["l1_subln_g"] = gain(V_DIM)
    inp["l1_w_o"] = nrm((N_HEADS * V_DIM, D_MODEL), (N_HEADS * V_DIM) ** -0.5 * DEEPNORM_BETA)
    inp["l1_ln2_g"], inp["l1_ln2_b"] = gain(D_MODEL), bias(D_MODEL)
    inp["l1_ffn2_w_in"], inp["l1_ffn2_w_out"] = ffn()
    inp["l1_ln3_g"], inp["l1_ln3_b"] = gain(D_MODEL), bias(D_MODEL)
    return inp


def reference(x,
              l0_ffn1_w_in, l0_ffn1_w_out, l0_ln1_g, l0_ln1_b,
              l0_pool_w, l0_pool_scale, l0_ln2_g, l0_ln2_b,
              l0_ffn2_w_in, l0_ffn2_w_out, l0_ln3_g, l0_ln3_b,
              l1_ffn1_w_in, l1_ffn1_w_out, l1_ln1_g, l1_ln1_b,
              l1_w_qkv, l1_lam_q1, l1_lam_k1, l1_lam_q2, l1_lam_k2, l1_subln_g, l1_w_o,
              l1_ln2_g, l1_ln2_b,
              l1_ffn2_w_in, l1_ffn2_w_out, l1_ln3_g, l1_ln3_b):
    layers = [
        dict(ffn1=(l0_ffn1_w_in, l0_ffn1_w_out), ln1=(l0_ln1_g, l0_ln1_b),
             mix=(l0_pool_w, l0_pool_scale), ln2=(l0_ln2_g, l0_ln2_b),
             ffn2=(l0_ffn2_w_in, l0_ffn2_w_out), ln3=(l0_ln3_g, l0_ln3_b)),
        dict(ffn1=(l1_ffn1_w_in, l1_ffn1_w_out), ln1=(l1_ln1_g, l1_ln1_b),
             mix=(l1_w_qkv, l1_lam_q1, l1_lam_k1, l1_lam_q2, l1_lam_k2, l1_subln_g, l1_w_o),
             ln2=(l1_ln2_g, l1_ln2_b),
             ffn2=(l1_ffn2_w_in, l1_ffn2_w_out), ln3=(l1_ln3_g, l1_ln3_b)),
    ]
    for i in range(DEPTH):
        p = layers[i]
        x = deepnorm(x, FFN_RES * swiglu(x, *p["ffn1"]), *p["ln1"])
        if i % N_MIXERS == 0:
            y = pool_mixer(x, *p["mix"])
        else:
            lambda_init = 0.8 - 0.6 * math.exp(-0.3 * i)
            y = diff_attention(x, *p["mix"], lambda_init)
        x = deepnorm(x, y, *p["ln2"])
        x = deepnorm(x, FFN_RES * swiglu(x, *p["ffn2"]), *p["ln3"])
    return x
```

```python
import math
import numpy as np
import concourse.bass as bass
import concourse.mybir as mybir
from concourse.bass_utils import run_bass_kernel_spmd

F32 = mybir.dt.float32
BF16 = mybir.dt.bfloat16
AF = mybir.ActivationFunctionType
ALU = mybir.AluOpType

D = 1024
DFF = 2816
NFC = 22
NH = 8
TT = 512
ALPHA = (2.0 * 2) ** 0.25
LN_EPS_P = 1e-5 / (ALPHA * ALPHA)
RMS_EPS = 1e-5
LAMBDA_INIT = 0.8 - 0.6 * math.exp(-0.3 * 1)
ATTN_SCALE = 64 ** -0.5
POOL_W = (2, 4, 8, 16)
NS = 6
KG = 4096
WIN_H = tuple(int(160 * 2 ** (h + 1)) for h in range(8))
GH = (128, 256, 256, 512, 512, 512, 512, 512)

ENGS = ("pe", "act", "dve", "pool", "sp")
DEBUG_STAGE = 6
DIAG_SBUF = True


class Buf:
    __slots__ = ("name", "w", "r", "const")

    def __init__(self, name, const=False):
        self.name = name
        self.w = None
        self.r = {}
        self.const = const


class DSem:
    def __init__(self, name):
        self.name = name
        self.count = 0
        self.h = None


class Prog:
    def __init__(self):
        self.ops = {e: [] for e in ENGS}
        self.plan = True
        self.dsems = []

    def dsem(self, name):
        s = DSem(name)
        self.dsems.append(s)
        return s

    def op(self, eng, emit, reads=(), writes=(), dsem=None):
        if self.plan:
            return
        deps = {}

        def add(tok):
            if tok is None:
                return
            k, v = tok
            if deps.get(k, -1) < v:
                deps[k] = v

        for b in reads:
            add(b.w)
        for b in writes:
            add(b.w)
            for k, v in b.r.items():
                add((k, v))
        idx = len(self.ops[eng])
        if dsem is not None:
            dsem.count += 16
            tok = (("d", dsem), dsem.count)
        else:
            tok = (("e", eng), idx)
        for b in writes:
            b.w = tok
            b.r = {}
        for b in reads:
            if b.const or b in writes:
                continue
            k, v = tok
            if b.r.get(k, -1) < v:
                b.r[k] = v
        self.ops[eng].append([emit, deps, dsem, False])

    def emit_all(self, nc):
        for e in ENGS:
            for o in self.ops[e]:
                for (kind, who), v in o[1].items():
                    if kind == "e" and not (who == "pe" and e == "pe"):
                        self.ops[who][v][3] = True
        cum = {}
        for e in ENGS:
            c = 0
            arr = []
            for o in self.ops[e]:
                if o[3] and o[2] is None:
                    c += 1
                arr.append(c)
            cum[e] = arr
        return cum


def _build(S, plan_only=False):
    NT = S // TT
    NB = S // 128
    nc = bass.Bass("TRN2", target_bir_lowering=False)

    def din(name, shape, dt=F32):
        return nc.dram_tensor(name, list(shape), dt, kind="ExternalInput").ap()

    def dscr(name, shape, dt=BF16):
        return nc.dram_tensor(name, list(shape), dt).ap()

    xT = din("xT", [D, S])
    outT = nc.dram_tensor("outT", [D, S], F32, kind="ExternalOutput").ap()
    ffn_names = ["l0f1", "l0f2", "l1f1", "l1f2"]
    w_in_f = {n: din(n + "_win", [NFC * 128, 2048]) for n in ffn_names}
    w_out_f = {n: din(n + "_wout", [8 * 128, DFF]) for n in ffn_names}
    w_q_f = din("wq", [2 * 128, 4096])
    w_k_f = din("wk", [2 * 128, 4096])
    w_v_f = din("wv", [2 * 128, 4096])
    w_o_f = din("wo", [2 * 128, 4096])
    w_p_f = din("wp", [128, 2048])
    vecs_d = din("vecs", [128, 13 * 8])
    subg_d = din("subg", [128, 1])
    lam_d = din("lam", [1, 256])

    w_in_b = {n: dscr(n + "_win_b", [NFC * 128, 2048]) for n in ffn_names}
    w_out_b = {n: dscr(n + "_wout_b", [8 * 128, DFF]) for n in ffn_names}
    w_q_b = dscr("wq_b", [2 * 128, 4096])
    w_k_b = dscr("wk_b", [2 * 128, 4096])
    w_v_b = dscr("wv_b", [2 * 128, 4096])
    w_o_b = dscr("wo_b", [2 * 128, 4096])
    w_p_b = dscr("wp_b", [128, 2048])
    if DEBUG_STAGE >= 50:
        ktc = nc.dram_tensor("ktc", [NH, 128, S], BF16, kind="ExternalOutput").ap()
        vc = nc.dram_tensor("vc", [NH, 128, NB, 128], BF16, kind="ExternalOutput").ap()
    else:
        ktc = dscr("ktc", [NH, 128, S])
        vc = dscr("vc", [NH, 128, NB, 128])

    P = Prog()
    dbg_d = nc.dram_tensor("dbg", [128, 8, TT], F32, kind="ExternalOutput").ap() if DEBUG_STAGE >= 50 else None

    sb = {}

    def salloc(name, shape, dt):
        t = nc.alloc_sbuf_tensor(name, list(shape), dt)
        sb[name] = t
        return t

    XS = [salloc("X%d" % i, [128, 8, 16 + TT], F32) for i in range(2)]
    Z = salloc("Z", [128, 8, TT], F32)
    XBFS = [salloc("XBF%d" % i, [128, 8, TT], BF16) for i in range(2)]
    ATMP = salloc("ATMP", [128, 4, TT], F32)
    HILO = salloc("HILO", [128, 2, TT], BF16)
    H = salloc("H", [128, NFC, TT], BF16)
    ZB = salloc("ZB", [128, 8, TT], BF16)
    ZSQ = salloc("ZSQ", [128, 8, TT], BF16)
    TA = salloc("TA", [128, 2, 16 + TT], F32)
    TB = salloc("TB", [128, 2, 16 + TT], F32)
    SG = salloc("SG", [128, 2, TT], F32)
    LNT = salloc("LNT", [128, 4, TT], F32)
    QT = salloc("QT", [128, 8, TT], BF16)
    PT = salloc("PT", [128, 4, TT], BF16)
    ON = salloc("ON", [128, 8, TT], BF16)
    OSQ = salloc("OSQ", [128, TT], BF16)
    RING = salloc("RING", [128, NS, 4096], BF16)
    ONES = salloc("ONES", [128, 128], BF16)
    TRI = salloc("TRI", [128, 128], BF16)
    VECS = salloc("VECS", [128, 13, 8], F32)
    PSA = salloc("PSA", [128, 8], F32)
    SUBG = salloc("SUBG", [128, 1], F32)
    LAMT = salloc("LAMT", [128, 4, 64], F32)
    LAMS = salloc("LAMS", [128, 8], F32)
    BTG = salloc("BTG", [128, NH, NB + 3], F32)
    IOT = salloc("IOT", [128, NB + 3], F32)
    RC = salloc("RC", [128, 4, 16], F32)
    RCT = salloc("RCT", [128, 16], F32)
    DBG = salloc("DBG", [128, 8, TT], F32) if DEBUG_STAGE >= 50 else None
    bDBG = Buf("DBG")
    PSB = [nc.alloc_psum_tensor("ps%d" % i, [128, TT], F32) for i in range(8)]

    bXS = [[Buf("X%d_%d" % (i, c)) for c in range(8)] for i in range(2)]
    bZ = [Buf("Z%d" % c) for c in range(8)]
    bXBFS = [[Buf("XBF%d_%d" % (i, c)) for c in range(8)] for i in range(2)]
    bATMP = [Buf("ATMP%d" % c) for c in range(4)]

    class Ctx:
        pass
    CTX = []
    for i in range(2):
        C = Ctx()
        C.i, C.X, C.XBF, C.bX, C.bXBF = i, XS[i], XBFS[i], bXS[i], bXBFS[i]
        CTX.append(C)
    bH = [Buf("H%d" % c) for c in range(NFC)]
    bZB = [Buf("ZB%d" % c) for c in range(8)]
    bZSQ = [Buf("ZSQ%d" % c) for c in range(8)]
    bTA = [Buf("TA%d" % c) for c in range(2)]
    bTB = [Buf("TB%d" % c) for c in range(2)]
    bSG = [Buf("SG%d" % c) for c in range(2)]
    bLNT = [Buf("LNT%d" % c) for c in range(4)]
    bQT = [Buf("QT%d" % c) for c in range(8)]
    bPT = [Buf("PT%d" % c) for c in range(4)]
    bON = [Buf("ON%d" % c) for c in range(8)]
    bOSQ = Buf("OSQ")
    bHILO = Buf("HILO")
    bRING = [Buf("RING%d" % c) for c in range(NS)]
    bPS = [Buf("PS%d" % c) for c in range(8)]
    bCONST = Buf("CONST", const=True)
    bRCT = Buf("RCT")
    bKTC = Buf("KTC")
    bVC = Buf("VC")
    bOUT = Buf("OUT")
    bOUTS = [Buf("OUT0"), Buf("OUT1")]
    bW = {n: Buf("W_" + n) for n in
          [f + sfx for f in ("l0f1", "l0f2", "l1f1", "l1f2") for sfx in ("_win", "_wout")] + ["wq", "wk", "wv", "wo", "wp"]}
    sem_ring = [P.dsem("ring%d" % i) for i in range(NS)]
    sem_xs = [P.dsem("xload%d" % i) for i in range(2)]
    sem_outs = [P.dsem("out%d" % i) for i in range(2)]
    sem_out = sem_outs[0]
    sem_kw = P.dsem("kw")
    sem_vw = P.dsem("vw")
    sem_const = P.dsem("const")
    sem_cast = {}
    bWp = {}
    bCHAIN = Buf("CASTCHAIN")

    def cast(name, dst, src):
        b = bW[name]
        s = P.dsem("cast_" + name)
        sem_cast[name] = s
        P.op("pool", lambda e, dst=dst, src=src: e.dma_start(out=dst, in_=src), writes=[b, bCHAIN], dsem=s)

    def castp(name, dst, src, k):
        b = Buf("W_%s_%d" % (name, k))
        bWp[(name, k)] = b
        sm = P.dsem("castp_%s_%d" % (name, k))
        P.op("pool", lambda e, dst=dst, src=src: e.dma_start(out=dst, in_=src), writes=[b, bW[name], bCHAIN], dsem=sm)

    def prologue():
        P.op("sp", lambda e: e.dma_start(out=VECS[:], in_=vecs_d.rearrange("p (k c) -> p k c", k=13)),
             writes=[bCONST], dsem=sem_const)
        P.op("sp", lambda e: e.dma_start(out=SUBG[:], in_=subg_d), writes=[bCONST], dsem=sem_const)
        P.op("sp", lambda e: e.dma_start(out=LAMT[:].rearrange("p a b -> p (a b)"),
                                        in_=lam_d.to_broadcast([128, 256])),
             writes=[bCONST], dsem=sem_const)
        P.op("dve", lambda e: e.memset(ONES[:], 1.0), writes=[bCONST])
        for i in range(2):
            P.op("dve", lambda e, i=i: e.memset(XS[i][:].rearrange("p a b -> p (a b)"), 0.0), writes=bXS[i])
        P.op("pool", lambda e: e.memset(TRI[:], 1.0), writes=[bCONST])
        P.op("pool", lambda e: e.affine_select(out=TRI[:], in_=TRI[:], pattern=[[1, 128]],
                                               compare_op=ALU.is_ge, fill=0.0, base=0, channel_multiplier=-1),
             reads=[bCONST], writes=[bCONST])
        P.op("pool", lambda e: e.iota(IOT[:], [[-128, NB + 3]], base=384, channel_multiplier=1,
                                      allow_small_or_imprecise_dtypes=True), writes=[bCONST])
        for h in range(NH):
            slope = 2.0 ** (-(h + 1))
            P.op("pool", lambda e, h=h, slope=slope: e.tensor_scalar(BTG[:, h, :], IOT[:], float(1 - GH[h]), slope,
                                                                      ALU.add, ALU.mult),
                 reads=[bCONST], writes=[bCONST])
        for wi, w in enumerate(POOL_W):
            P.op("dve", lambda e, wi=wi, w=w: e.memset(RC[:, wi, :], 1.0 / w), writes=[bCONST])
            for t in range(w - 1):
                P.op("dve", lambda e, wi=wi, t=t: e.memset(RC[:, wi, t:t + 1], 1.0 / (t + 1)), writes=[bCONST])
        P.op("dve", lambda e: e.tensor_scalar(PSA[:], VECS[:, 12, :], 1.0 / ALPHA, None, ALU.mult),
             reads=[bCONST], writes=[bCONST])
        P.op("dve", lambda e: e.tensor_scalar(SUBG[:], SUBG[:], 1.0 - LAMBDA_INIT, None, ALU.mult),
             reads=[bCONST], writes=[bCONST])
        P.op("dve", lambda e: e.tensor_tensor(LAMT[:, 0, :], LAMT[:, 0, :], LAMT[:, 1, :], ALU.mult),
             reads=[bCONST], writes=[bCONST])
        P.op("dve", lambda e: e.tensor_tensor(LAMT[:, 2, :], LAMT[:, 2, :], LAMT[:, 3, :], ALU.mult),
             reads=[bCONST], writes=[bCONST])
        P.op("dve", lambda e: e.reduce_sum(LAMS[:, 0:1], LAMT[:, 0, :], axis=mybir.AxisListType.X),
             reads=[bCONST], writes=[bCONST])
        P.op("dve", lambda e: e.reduce_sum(LAMS[:, 1:2], LAMT[:, 2, :], axis=mybir.AxisListType.X),
             reads=[bCONST], writes=[bCONST])
        P.op("act", lambda e: e.activation(LAMS[:, 2:4], LAMS[:, 0:2], AF.Exp), reads=[bCONST], writes=[bCONST])
        P.op("dve", lambda e: e.tensor_tensor(LAMS[:, 4:5], LAMS[:, 2:3], LAMS[:, 3:4], ALU.subtract),
             reads=[bCONST], writes=[bCONST])
        P.op("dve", lambda e: e.tensor_scalar(LAMS[:, 5:6], LAMS[:, 4:5], LAMBDA_INIT, -1.0, ALU.add, ALU.mult),
             reads=[bCONST], writes=[bCONST])
        order = []
        for n in ffn_names[:2]:
            order.append((n + "_win", w_in_b[n], w_in_f[n]))
            order.append((n + "_wout", w_out_b[n], w_out_f[n]))
            if n == "l0f1":
                order.append(("wp", w_p_b, w_p_f))
        n = "l1f1"
        order.append((n + "_win", w_in_b[n], w_in_f[n]))
        order.append((n + "_wout", w_out_b[n], w_out_f[n]))
        order += [("wq", w_q_b, w_q_f), ("wk", w_k_b, w_k_f), ("wv", w_v_b, w_v_f), ("wo", w_o_b, w_o_f)]
        n = "l1f2"
        order.append((n + "_win", w_in_b[n], w_in_f[n]))
        order.append((n + "_wout", w_out_b[n], w_out_f[n]))
        for name, dst, src in order:
            if name == "l0f1_win":
                for jj in range(NFC // 2):
                    castp(name, dst[jj * 256:(jj + 1) * 256, :], src[jj * 256:(jj + 1) * 256, :], jj)
            elif name == "l0f1_wout":
                for c in range(8):
                    castp(name, dst[c * 128:(c + 1) * 128, :], src[c * 128:(c + 1) * 128, :], c)
            else:
                cast(name, dst, src)

    plan = []
    state = {"issued": 0, "consumed": 0}
    fired = set()
    closed = set()

    def ring_issue():
        while state["issued"] < len(plan):
            i = state["issued"]
            if i >= NS and (i - NS) not in closed:
                break
            mk, reads, ev = plan[i]
            if ev is not None and ev not in fired:
                break
            if callable(reads):
                reads = reads()
            s = i % NS
            P.op("sp", mk(s), reads=reads, writes=[bRING[s]], dsem=sem_ring[s])
            state["issued"] += 1

    def ring_next(mk, reads, ev=None):
        if P.plan:
            plan.append((mk, reads, ev))
            return len(plan) - 1, 0
        i = state["consumed"]
        state["consumed"] += 1
        ring_issue()
        assert state["issued"] > i, "ring item %d not issued (event %s)" % (i, plan[i][2])
        return i, i % NS

    def ring_close(i):
        if P.plan:
            return
        closed.add(i)
        ring_issue()

    def ld_rows(src, r0, ncols):
        def mk(s):
            return lambda e: e.dma_start(out=RING[:, s, 0:ncols], in_=src[r0:r0 + 128, 0:ncols])
        return mk

    def mm(out, lhsT, rhs, start, stop, reads, writes):
        P.op("pe", lambda e: e.matmul(out, lhsT, rhs, start=start, stop=stop), reads=reads, writes=writes)

    TQ = []

    def tick(n=1):
        for _ in range(n):
            if TQ:
                TQ.pop(0)()

    def flush():
        tick(len(TQ))

    def ln_steps(C, gk, bk):
        X, XBF, bX, bXBF = C.X, C.XBF, C.bX, C.bXBF
        MEAN, MSQ, VAR, RSTD = (LNT[:, i, :] for i in range(4))
        LNV = MSQ
        steps = []

        def sq(c0):
            def f():
                for c in (c0, c0 + 1):
                    P.op("act", lambda e, c=c: e.activation(ZSQ[:, c, :], Z[:, c, :], AF.Square),
                         reads=[bZ[c]], writes=[bZSQ[c]])
                    P.op("act", lambda e, c=c: e.activation(ZB[:, c, :], Z[:, c, :], AF.Copy), reads=[bZ[c]], writes=[bZB[c]])
            return f
        for c0 in (0, 2, 4, 6):
            steps.append(sq(c0))

        def st1():
            for c in range(8):
                mm(PSB[6][:], ONES[:], ZB[:, c, :], c == 0, c == 7, [bZB[c], bCONST], [bPS[6]])

        def st2():
            for c in range(8):
                mm(PSB[7][:], ONES[:], ZSQ[:, c, :], c == 0, c == 7, [bZSQ[c], bCONST], [bPS[7]])

        def st3():
            P.op("dve", lambda e: e.tensor_scalar(MEAN, PSB[6][:], 1.0 / D, None, ALU.mult),
                 reads=[bPS[6]], writes=[bLNT[0]])
            P.op("dve", lambda e: e.tensor_tensor(MSQ, MEAN, MEAN, ALU.mult), reads=[bLNT[0]], writes=[bLNT[1]])
            P.op("dve", lambda e: e.scalar_tensor_tensor(VAR, PSB[7][:], 1.0 / D, MSQ, ALU.mult, ALU.subtract),
                 reads=[bPS[7], bLNT[1]], writes=[bLNT[2]])
            P.op("act", lambda e: e.activation(LNV, VAR, AF.Ln, bias=EPSC[:, 0:1]), reads=[bLNT[2], bCONST], writes=[bLNT[1]])
            P.op("act", lambda e: e.activation(RSTD, LNV, AF.Exp, scale=-0.5), reads=[bLNT[1]], writes=[bLNT[3]])
        steps += [st1, st2, st3]

        def nz(c):
            def f():
                P.op("dve", lambda e: e.tensor_tensor(Z[:, c, :], Z[:, c, :], MEAN, ALU.subtract),
                     reads=[bZ[c], bLNT[0]], writes=[bZ[c]])
                P.op("dve", lambda e: e.tensor_tensor(Z[:, c, :], Z[:, c, :], RSTD, ALU.mult),
                     reads=[bZ[c], bLNT[3]], writes=[bZ[c]])
                P.op("dve", lambda e: e.tensor_scalar(XBF[:, c, :], Z[:, c, :], VECS[:, gk, c:c + 1],
                                                      VECS[:, bk, c:c + 1], ALU.mult, ALU.add),
                     reads=[bZ[c], bCONST], writes=[bXBF[c]])
                P.op("act", lambda e: e.activation(X[:, c, 16:], Z[:, c, :], AF.Identity,
                                                   bias=VECS[:, bk, c:c + 1], scale=VECS[:, gk, c:c + 1]),
                     reads=[bZ[c], bCONST], writes=[bX[c]])
            return f
        for c in range(8):
            steps.append(nz(c))
        return steps

    def ffn(C, name):
        X, XBF, bX, bXBF = C.X, C.XBF, C.bX, C.bXBF
        wi, wo = w_in_b[name], w_out_b[name]
        bwi, bwo = bW[name + "_win"], bW[name + "_wout"]
        for jj in range(NFC // 2):
            def mk(s, jj=jj):
                return lambda e: e.dma_start(
                    out=RING[:, s, :].rearrange("p (u f) -> p u f", u=2),
                    in_=wi[jj * 256:(jj + 1) * 256, :].rearrange("(u p) f -> p u f", u=2))
            it, s = ring_next(mk, lambda jj=jj: [bWp.get((name + "_win", jj), bwi)])
            for u in range(2):
                j = 2 * jj + u
                G, U = PSB[j % 2], PSB[2 + j % 2]
                base = u * 2048
                for c in range(8):
                    mm(G[:], RING[:, s, base + c * 256: base + c * 256 + 128], XBF[:, c, :], c == 0, c == 7,
                       [bRING[s], bXBF[c]], [bPS[j % 2]])
                for c in range(8):
                    mm(U[:], RING[:, s, base + c * 256 + 128: base + c * 256 + 256], XBF[:, c, :], c == 0, c == 7,
                       [bRING[s], bXBF[c]], [bPS[2 + j % 2]])
                P.op("act", lambda e, j=j, G=G: e.activation(SG[:, j % 2, :], G[:], AF.Silu),
                     reads=[bPS[j % 2]], writes=[bSG[j % 2]])
                P.op("dve", lambda e, j=j, U=U: e.tensor_tensor(H[:, j, :], SG[:, j % 2, :], U[:], ALU.mult),
                     reads=[bSG[j % 2], bPS[2 + j % 2]], writes=[bH[j]])
                tick(2)
            ring_close(it)
        flush()
        for c in range(8):
            it, s = ring_next(ld_rows(wo, c * 128, DFF), lambda c=c: [bWp.get((name + "_wout", c), bwo)])
            Y = PSB[4 + c % 2]
            for j in range(NFC):
                mm(Y[:], RING[:, s, j * 128:(j + 1) * 128], H[:, j, :], j == 0, j == NFC - 1,
                   [bRING[s], bH[j]], [bPS[4 + c % 2]])
            ring_close(it)
            P.op("dve", lambda e, c=c, Y=Y: e.scalar_tensor_tensor(Z[:, c, :], Y[:], 0.5 / ALPHA, X[:, c, 16:],
                                                                   ALU.mult, ALU.add),
                 reads=[bPS[4 + c % 2], bX[c]], writes=[bZ[c]])
            tick()

    def pool_steps(C, Cn, first_tile):
        X, XBF, bX, bXBF = C.X, C.XBF, C.bX, C.bXBF
        DD, bDD = ZB, bZB
        steps = []
        st = {}

        def chunk(c):
            def f():
                w = POOL_W[c // 2]
                k = c % 2
                eng = "pool" if c % 2 == 0 else "dve"
                nlev = int(math.log2(w))
                cur = None
                for lev in range(1, nlev + 1):
                    sh = 1 << (lev - 1)
                    lo = 16 - (w - (1 << lev))
                    dstT, bdst = (TA, bTA) if lev % 2 == 1 else (TB, bTB)
                    if lev == 1:
                        in0, in1 = X[:, c, lo:], X[:, c, lo - sh:16 + TT - sh]
                        rd = [bX[c]]
                    else:
                        pT, bp = (TB, bTB) if lev % 2 == 1 else (TA, bTA)
                        in0, in1 = pT[:, k, lo:], pT[:, k, lo - sh:16 + TT - sh]
                        rd = [bp[k]]
                    P.op(eng, lambda e, dstT=dstT, k=k, lo=lo, in0=in0, in1=in1: e.tensor_tensor(
                        dstT[:, k, lo:], in0, in1, ALU.add), reads=rd, writes=[bdst[k]])
                    cur, bcur = dstT, bdst
                P.op("dve", lambda e, c=c, cur=cur, k=k, w=w: e.scalar_tensor_tensor(
                    DD[:, c, :], cur[:, k, 16:], 1.0 / w, X[:, c, 16:], ALU.mult, ALU.subtract),
                    reads=[bcur[k], bX[c]], writes=[bDD[c]])
                if first_tile:
                    wi = c // 2
                    P.op("dve", lambda e, cur=cur, k=k, wi=wi: e.tensor_tensor(RCT[:], cur[:, k, 16:32], RC[:, wi, :], ALU.mult),
                         reads=[bcur[k], bCONST], writes=[bRCT])
                    P.op("dve", lambda e, c=c: e.tensor_tensor(DD[:, c, 0:16], RCT[:], X[:, c, 16:32], ALU.subtract),
                         reads=[bRCT, bX[c]], writes=[bDD[c]])
            return f
        for c in range(8):
            steps.append(chunk(c))

        def halo():
            for c in range(8):
                P.op("pool", lambda e, c=c: e.tensor_copy(Cn.X[:, c, 0:16], X[:, c, TT:TT + 16]), reads=[bX[c]], writes=[Cn.bX[c]])
        steps.append(halo)

        def grp(g):
            def f():
                if g == 0:
                    st["it"], st["s"] = ring_next(ld_rows(w_p_b, 0, 2048), [bW["wp"]])
                s_ = st["s"]
                for oc in range(2):
                    c = 2 * g + oc
                    Y = PSB[4 + c % 2]
                    for ic in range(2):
                        col = ((g * 2 + oc) * 2 + ic) * 128
                        mm(Y[:], RING[:, s_, col:col + 128], DD[:, 2 * g + ic, :], ic == 0, ic == 1,
                           [bRING[s_], bDD[2 * g + ic]], [bPS[4 + c % 2]])
                    P.op("dve", lambda e, c=c, Y=Y: e.scalar_tensor_tensor(Z[:, c, :], Y[:], PSA[:, c:c + 1], X[:, c, 16:],
                                                                           ALU.mult, ALU.add),
                         reads=[bPS[4 + c % 2], bX[c], bCONST], writes=[bZ[c]])
                if g == 3:
                    ring_close(st["it"])
            return f
        for g in range(4):
            steps.append(grp(g))
        return steps

    def qkv(C, j):
        X, XBF, bX, bXBF = C.X, C.XBF, C.bX, C.bXBF
        t0 = j * TT
        KTt, Vt = H[:, 0:8, :], H[:, 8:16, :]
        bZB, bZSQ = bH[0:8], bH[8:16]
        for which, wsrc, bname in (("q", w_q_b, "wq"), ("k", w_k_b, "wk")):
            for grp in range(2):
                it, s = ring_next(ld_rows(wsrc, grp * 128, 4096), [bW[bname]])
                for hh in range(4):
                    h = grp * 4 + hh
                    pb = h % 4
                    for c in range(8):
                        col = (hh * 8 + c) * 128
                        mm(PSB[pb][:], RING[:, s, col:col + 128], XBF[:, c, :], c == 0, c == 7,
                           [bRING[s], bXBF[c]], [bPS[pb]])
                    if which == "q":
                        P.op("act", lambda e, h=h, pb=pb: e.activation(QT[:, h, :], PSB[pb][:], AF.Copy),
                             reads=[bPS[pb]], writes=[bQT[h]])
                    else:
                        P.op("dve", lambda e, h=h, pb=pb: e.tensor_copy(KTt[:, h, :], PSB[pb][:]),
                             reads=[bPS[pb]], writes=[bZB[h]])
                    tick()
                ring_close(it)
        P.op("sp", lambda e: e.dma_start(out=ktc.rearrange("h p t -> p h t")[:, :, t0:t0 + TT], in_=KTt),
             reads=bZB, writes=[bKTC], dsem=sem_kw)
        for half in range(2):
            it, s = ring_next(ld_rows(w_v_b, half * 128, 4096), [bW["wv"]])
            for sub in range(4):
                pb = (half * 4 + sub) % 4
                for c in range(8):
                    mm(PSB[pb][:], XBF[:, c, sub * 128:(sub + 1) * 128], RING[:, s, c * 512:(c + 1) * 512],
                       c == 0, c == 7, [bRING[s], bXBF[c]], [bPS[pb]])
                dst = Vt.rearrange("p (s a) t -> p s (a t)", s=4)[:, sub, half * 512:(half + 1) * 512]
                if sub % 2 == 0:
                    P.op("act", lambda e, dst=dst, pb=pb: e.activation(dst, PSB[pb][:], AF.Copy),
                         reads=[bPS[pb]], writes=[bZSQ[2 * sub], bZSQ[2 * sub + 1]])
                else:
                    P.op("dve", lambda e, dst=dst, pb=pb: e.tensor_copy(dst, PSB[pb][:]),
                         reads=[bPS[pb]], writes=[bZSQ[2 * sub], bZSQ[2 * sub + 1]])
                tick()
            ring_close(it)
        Vv = Vt.rearrange("p (s a) t -> p s (a t)", s=4)
        for h in range(NH):
            P.op("sp", lambda e, h=h: e.dma_start(out=vc[h, :, 4 * j:4 * j + 4, :], in_=Vv[:, :, h * 128:(h + 1) * 128]),
                 reads=bZSQ, writes=[bVC], dsem=sem_vw)
        fired.add(("kv", j))

    def attention(j):
        nkb = 4 * (j + 1)
        KB = KG // 128
        deferred = []

        def run_deferred(n):
            for _ in range(n):
                if deferred:
                    deferred.pop(0)()

        for h in range(NH):
            G = GH[h]
            gs = G // 128
            O1, O2 = PSB[4], PSB[5]
            a = h % 2
            ginfo = {}
            bmin = max(0, (j * TT - WIN_H[h]) // 128)

            def get_group(g, h=h, ginfo=ginfo, bmin=bmin):
                if g not in ginfo:
                    b0 = bmin + g * KB
                    b1 = min(4 * j if DIAG_SBUF else nkb, b0 + KB)
                    nk = (b1 - b0) * 128

                    def mkk(s, h=h, b0=b0, nk=nk):
                        return lambda e: e.dma_start(out=RING[:, s, 0:nk], in_=ktc[h, :, b0 * 128:b0 * 128 + nk])

                    def mkv(s, h=h, b0=b0, b1=b1, nk=nk):
                        return lambda e: e.dma_start(out=RING[:, s, 0:nk].rearrange("p (b e) -> p b e", e=128),
                                                     in_=vc[h, :, b0:b1, :])
                    ki, sk = ring_next(mkk, [bKTC], ("kv", j - 1 if DIAG_SBUF else j))
                    vi, sv = ring_next(mkv, [bVC], ("kv", j - 1 if DIAG_SBUF else j))
                    ginfo[g] = dict(ki=ki, sk=sk, vi=vi, sv=sv, b0=b0, b1=b1)
                return ginfo[g]

            def s_stage(b, h=h, G=G, gs=gs, bmin=bmin):
                c0 = max(0, b - 4 * j)
                q0 = 128 * c0
                i2 = b % 2
                if DIAG_SBUF and b >= 4 * j:
                    k1 = H[0:64, h, c0 * 128:(c0 + 1) * 128]
                    k2 = H[64:128, h, c0 * 128:(c0 + 1) * 128]
                    kb_ = bH[h]
                    gi = None
                else:
                    gi = get_group((b - bmin) // KB)
                    bl = b - gi["b0"]
                    sk = gi["sk"]
                    k1 = RING[0:64, sk, bl * 128:(bl + 1) * 128]
                    k2 = RING[64:128, sk, bl * 128:(bl + 1) * 128]
                    kb_ = bRING[sk]
                mm(PSB[i2][:, q0:], k1, QT[0:64, h, q0:], True, True, [kb_, bQT[h]], [bPS[i2]])
                mm(PSB[2 + i2][:, q0:], k2, QT[64:128, h, q0:], True, True, [kb_, bQT[h]], [bPS[2 + i2]])
                if gi is not None and b == gi["b1"] - 1:
                    ring_close(gi["ki"])
                for mp in range(2):
                    for gq in range(TT // G):
                        lo = max(q0, gq * G)
                        hi = (gq + 1) * G
                        if lo >= hi:
                            continue
                        n = 4 * j + gq * gs - b
                        P.op("act", lambda e, mp=mp, i2=i2, lo=lo, hi=hi, n=n, h=h: e.activation(
                            PT[:, 2 * mp + i2, lo:hi], PSB[2 * mp + i2][:, lo:hi], AF.Exp,
                            bias=BTG[:, h, n + 3:n + 4], scale=ATTN_SCALE),
                            reads=[bPS[2 * mp + i2], bCONST], writes=[bPT[2 * mp + i2]])
                if b >= 4 * j:
                    for mp in range(2):
                        P.op("dve", lambda e, mp=mp, i2=i2, q0=q0: e.tensor_tensor(
                            PT[:, 2 * mp + i2, q0:q0 + 128], PT[:, 2 * mp + i2, q0:q0 + 128], TRI[:], ALU.mult),
                            reads=[bPT[2 * mp + i2], bCONST], writes=[bPT[2 * mp + i2]])

            def o_stage(b, h=h, a=a, O1=O1, O2=O2, bmin=bmin):
                c0 = max(0, b - 4 * j)
                q0 = 128 * c0
                i2 = b % 2
                first, last = (b == bmin), (b == nkb - 1)
                if DIAG_SBUF and b >= 4 * j:
                    vblk = H[:, 8:16, :].rearrange("p (s a) t -> p s (a t)", s=4)[:, c0, h * 128:(h + 1) * 128]
                    vb_ = [bH[8 + 2 * c0], bH[8 + 2 * c0 + 1]]
                    gi = None
                else:
                    gi = get_group((b - bmin) // KB)
                    bl = b - gi["b0"]
                    sv = gi["sv"]
                    vblk = RING[:, sv, bl * 128:(bl + 1) * 128]
                    vb_ = [bRING[sv]]
                for mp, Oa in enumerate((O1, O2)):
                    mm(Oa[:, q0:], vblk, PT[:, 2 * mp + i2, q0:], first, last,
                       vb_ + [bPT[2 * mp + i2]], [bPS[4 + mp]])
                    if mp == 1:
                        mm(PSB[7][:, q0:], ONES[:], PT[:, 2 + i2, q0:], first, last,
                           [bCONST, bPT[2 + i2]], [bPS[7]])
                    elif first:
                        P.op("dve", lambda e, i2=i2: e.tensor_copy(ATMP[:, 3, :], PT[:, i2, :]),
                             reads=[bPT[i2]], writes=[bATMP[3]])
                    else:
                        P.op("dve", lambda e, i2=i2, q0=q0: e.tensor_tensor(ATMP[:, 3, q0:], ATMP[:, 3, q0:], PT[:, i2, q0:], ALU.add),
                             reads=[bPT[i2], bATMP[3]], writes=[bATMP[3]])
                if gi is not None and b == gi["b1"] - 1:
                    ring_close(gi["vi"])
            prev = None
            for b in range(bmin, nkb):
                s_stage(b)
                run_deferred(1)
                tick()
                if prev is not None:
                    o_stage(prev)
                prev = b
            o_stage(prev)
            run_deferred(len(deferred))

            R1, R2, T1, T2 = SG[:, 0, :], SG[:, 1, :], TA[:, 0, 0:TT], TA[:, 1, 0:TT]
            OO, OS1, OS2, SS1, SS2 = TB[:, 0, 0:TT], TB[:, 1, 0:TT], ATMP[:, 0, :], ATMP[:, 1, :], ATMP[:, 2, :]
            bZ = [bSG[0], bSG[1], bTA[0], bTA[1], bTB[0], bTB[1], bATMP[0], bATMP[1]]
            bACC = [bATMP[2]]
            P.op("dve", lambda e: e.tensor_copy(OS1, PSB[4][:]), reads=[bPS[4]], writes=[bZ[5]])
            P.op("act", lambda e: e.activation(OS2, PSB[5][:], AF.Copy), reads=[bPS[5]], writes=[bZ[6]])
            P.op("act", lambda e: e.activation(SS2, PSB[7][:], AF.Copy), reads=[bPS[7]], writes=[bACC[0]])

            def f1():
                P.op("act", lambda e: e.activation(HILO[:, 0, :], ATMP[:, 3, :], AF.Copy), reads=[bATMP[3]], writes=[bHILO])
                P.op("dve", lambda e: e.tensor_tensor(HILO[:, 1, :], ATMP[:, 3, :], HILO[:, 0, :], ALU.subtract),
                     reads=[bATMP[3], bHILO], writes=[bHILO])
                mm(PSB[6][:], ONES[:], HILO[:, 0, :], True, False, [bHILO, bCONST], [bPS[6]])
                mm(PSB[6][:], ONES[:], HILO[:, 1, :], False, True, [bHILO, bCONST], [bPS[6]])

            def f2():
                P.op("dve", lambda e: e.reciprocal(R1, PSB[6][:]), reads=[bPS[6]], writes=[bZ[0]])

            def f3():
                P.op("dve", lambda e: e.reciprocal(R2, SS2), reads=[bACC[0]], writes=[bZ[1]])

            def f4():
                P.op("dve", lambda e: e.tensor_tensor(T1, OS1, R1, ALU.mult), reads=[bZ[5], bZ[0]], writes=[bZ[2]])
                P.op("dve", lambda e: e.tensor_tensor(T2, OS2, R2, ALU.mult), reads=[bZ[6], bZ[1]], writes=[bZ[3]])

            def f5():
                P.op("dve", lambda e: e.scalar_tensor_tensor(OO, T2, LAMS[:, 5:6], T1, ALU.mult, ALU.add),
                     reads=[bZ[2], bZ[3], bCONST], writes=[bZ[4]])
                P.op("act", lambda e: e.activation(OSQ[:], OO, AF.Square), reads=[bZ[4]], writes=[bOSQ])

            def f6():
                mm(PSB[6][:], ONES[:], OSQ[:], True, True, [bOSQ, bCONST], [bPS[6]])
                P.op("act", lambda e: e.activation(R1, PSB[6][:], AF.Ln, bias=EPSC[:, 1:2], scale=1.0 / 128),
                     reads=[bPS[6], bCONST], writes=[bZ[0]])

            def f7():
                P.op("act", lambda e: e.activation(R2, R1, AF.Exp, scale=-0.5), reads=[bZ[0]], writes=[bZ[1]])

            def f8(h=h):
                P.op("dve", lambda e: e.tensor_tensor(T1, OO, R2, ALU.mult), reads=[bZ[4], bZ[1]], writes=[bZ[2]])
                P.op("dve", lambda e, h=h: e.tensor_scalar(ON[:, h, :], T1, SUBG[:, 0:1], None, ALU.mult),
                     reads=[bZ[2], bCONST], writes=[bON[h]])

            nop = lambda: None
            deferred.extend([f1, f2, nop, nop, f3, nop, nop, f4, f5, nop, nop, f6, nop, f7, nop, f8])
        run_deferred(len(deferred))

    def wo_proj(C):
        X, XBF, bX, bXBF = C.X, C.XBF, C.bX, C.bXBF
        flush()
        for grp in range(2):
            it, s = ring_next(ld_rows(w_o_b, grp * 128, 4096), [bW["wo"]])
            for cl in range(4):
                c = grp * 4 + cl
                Y = PSB[c % 2]
                for h in range(NH):
                    col = (cl * 8 + h) * 128
                    mm(Y[:], RING[:, s, col:col + 128], ON[:, h, :], h == 0, h == NH - 1,
                       [bRING[s], bON[h]], [bPS[c % 2]])
                P.op("dve", lambda e, c=c, Y=Y: e.scalar_tensor_tensor(Z[:, c, :], Y[:], 1.0 / ALPHA, X[:, c, 16:],
                                                                       ALU.mult, ALU.add),
                     reads=[bPS[c % 2], bX[c]], writes=[bZ[c]])
                tick()
            ring_close(it)

    def load_tile(C, j):
        X, XBF, bX, bXBF = C.X, C.XBF, C.bX, C.bXBF
        t0 = j * TT
        P.op("sp", lambda e: e.dma_start(out=X[:, :, 16:],
                                        in_=xT.rearrange("(c p) t -> p c t", p=128)[:, :, t0:t0 + TT]),
             writes=bX, dsem=sem_xs[C.i])
        for c in range(8):
            if c % 2 == 0:
                P.op("act", lambda e, c=c: e.activation(XBF[:, c, :], X[:, c, 16:], AF.Copy), reads=[bX[c]], writes=[bXBF[c]])
            else:
                P.op("dve", lambda e, c=c: e.tensor_copy(XBF[:, c, :], X[:, c, 16:]), reads=[bX[c]], writes=[bXBF[c]])

    def store_tile(C, j):
        t0 = j * TT
        P.op("sp", lambda e: e.dma_start(out=outT.rearrange("(c p) t -> p c t", p=128)[:, :, t0:t0 + TT],
                                        in_=C.X[:, :, 16:]),
             reads=C.bX, writes=[bOUTS[C.i]], dsem=sem_outs[C.i])

    def attn_stage(C, j):
        qkv(C, j)
        attention(j)
        wo_proj(C)

    STAGES = [
        (lambda C, j: ffn(C, "l0f1"),
         lambda C, j: ln_steps(C, 0, 1) + pool_steps(C, CTX[1 - C.i], j == 0) + ln_steps(C, 2, 3)),
        (lambda C, j: ffn(C, "l0f2"), lambda C, j: ln_steps(C, 4, 5)),
        (lambda C, j: ffn(C, "l1f1"), lambda C, j: ln_steps(C, 6, 7)),
        (attn_stage, lambda C, j: ln_steps(C, 8, 9)),
        (lambda C, j: ffn(C, "l1f2"), lambda C, j: ln_steps(C, 10, 11)),
    ]

    def run_tiles():
        CA, CB = CTX
        pend = None
        for jA in range(0, NT, 2):
            jB = jA + 1 if jA + 1 < NT else None
            load_tile(CA, jA)
            for k, (S, TR) in enumerate(STAGES):
                S(CA, jA)
                flush()
                if k == 0:
                    if pend is not None:
                        store_tile(CB, pend)
                        pend = None
                    if jB is not None:
                        load_tile(CB, jB)
                TQ.extend(TR(CA, jA))
                if jB is not None:
                    S(CB, jB)
                    flush()
                    TQ.extend(TR(CB, jB))
                else:
                    flush()
            store_tile(CA, jA)
            pend = jB
        flush()
        if pend is not None:
            store_tile(CB, pend)

    EPSC = salloc("EPSC", [128, 2], F32)

    def program():
        if not P.plan:
            prologue()
            P.op("dve", lambda e: e.memset(EPSC[:, 0:1], LN_EPS_P), writes=[bCONST])
            P.op("dve", lambda e: e.memset(EPSC[:, 1:2], RMS_EPS), writes=[bCONST])
        run_tiles()

    P.plan = True
    program()
    fired.clear()
    P.plan = False
    program()
    assert state["consumed"] == len(plan) and state["issued"] == len(plan)
    cum = P.emit_all(nc)
    if plan_only:
        return nc, P

    import contextlib
    with contextlib.ExitStack() as es:
        esem = {e: es.enter_context(nc.semaphore("e_" + e)) for e in ENGS}
        for ds in P.dsems:
            ds.h = es.enter_context(nc.semaphore("d_" + ds.name))
        fin = es.enter_context(nc.semaphore("fin"))
        block = es.enter_context(nc.Block())

        def replay(ename, eng):
            clock = {}
            for idx, (emit, deps, dsem, marked) in enumerate(P.ops[ename]):
                for (kind, who), v in deps.items():
                    if kind == "e":
                        if who == "pe" and ename == "pe":
                            continue
                        sem, val = esem[who], cum[who][v]
                    else:
                        sem, val = who.h, v
                    key = (kind, who if kind == "e" else who.name)
                    if clock.get(key, 0) >= val:
                        continue
                    clock[key] = val
                    eng.wait_ge(sem, val)
                ins = emit(eng)
                if dsem is not None:
                    ins.then_inc(dsem.h, 16)
                elif marked:
                    ins.then_inc(esem[ename], 1)
            if ename == "sp":
                for so in sem_outs:
                    if so.count:
                        eng.wait_ge(so.h, so.count)

        @block.tensor
        def _(e):
            replay("pe", e)

        @block.scalar
        def _(e):
            replay("act", e)

        @block.vector
        def _(e):
            replay("dve", e)

        @block.gpsimd
        def _(e):
            replay("pool", e)

        @block.sync
        def _(e):
            replay("sp", e)
    return nc, P


def _prep_weights(inp):
    def r_win(w):
        g = w[:, :DFF].reshape(8, 128, NFC, 128)
        u = w[:, DFF:].reshape(8, 128, NFC, 128)
        o = np.stack([g, u], axis=3)
        o = o.transpose(2, 1, 0, 3, 4)
        return np.ascontiguousarray(o).reshape(NFC * 128, 2048)

    def r_wout(w):
        o = w.reshape(NFC, 128, 8, 128).transpose(2, 1, 0, 3)
        return np.ascontiguousarray(o).reshape(8 * 128, DFF)

    m = {}
    for n, pre in (("l0f1", "l0_ffn1"), ("l0f2", "l0_ffn2"), ("l1f1", "l1_ffn1"), ("l1f2", "l1_ffn2")):
        m[n + "_win"] = r_win(np.asarray(inp[pre + "_w_in"], np.float32))
        m[n + "_wout"] = r_wout(np.asarray(inp[pre + "_w_out"], np.float32))
    wqkv = np.asarray(inp["l1_w_qkv"], np.float32)

    def r_qk(off1, off2):
        a = wqkv[:, off1:off1 + 512].reshape(8, 128, 2, 4, 64)
        b = wqkv[:, off2:off2 + 512].reshape(8, 128, 2, 4, 64)
        o = np.concatenate([a, b], axis=4)
        o = o.transpose(2, 1, 3, 0, 4)
        return np.ascontiguousarray(o).reshape(2 * 128, 4096)

    m["wq"] = r_qk(0, 512)
    m["wk"] = r_qk(1024, 1536)
    v = wqkv[:, 2048:3072].reshape(8, 128, 2, 512).transpose(2, 1, 0, 3)
    m["wv"] = np.ascontiguousarray(v).reshape(2 * 128, 4096)
    wo = np.asarray(inp["l1_w_o"], np.float32).reshape(8, 128, 2, 4, 128)
    m["wo"] = np.ascontiguousarray(wo.transpose(2, 1, 3, 0, 4)).reshape(2 * 128, 4096)
    pw = np.asarray(inp["l0_pool_w"], np.float32).reshape(4, 2, 128, 2, 128)
    m["wp"] = np.ascontiguousarray(pw.transpose(2, 0, 3, 1, 4)).reshape(128, 2048)
    names = ["l0_ln1_g", "l0_ln1_b", "l0_ln2_g", "l0_ln2_b", "l0_ln3_g", "l0_ln3_b",
             "l1_ln1_g", "l1_ln1_b", "l1_ln2_g", "l1_ln2_b", "l1_ln3_g", "l1_ln3_b", "l0_pool_scale"]
    vs = np.stack([np.asarray(inp[k], np.float32).reshape(8, 128) for k in names], axis=0)
    m["vecs"] = np.ascontiguousarray(vs.transpose(2, 0, 1)).reshape(128, 13 * 8)
    m["subg"] = np.asarray(inp["l1_subln_g"], np.float32).reshape(128, 1).copy()
    m["lam"] = np.concatenate([np.asarray(inp[k], np.float32).reshape(64) for k in
                               ("l1_lam_q1", "l1_lam_k1", "l1_lam_q2", "l1_lam_k2")]).reshape(1, 256).copy()
    return m


_CACHE = {}


def run(inputs, S=None, n_cores=8, trace=False):
    x = np.asarray(inputs["x"], np.float32)
    B = x.shape[0]
    if S is None:
        S = x.shape[1]
    wm = _prep_weights(inputs)
    key = (S, DEBUG_STAGE)
    if key not in _CACHE:
        _CACHE[key] = _build(S)[0]
    nc = _CACHE[key]
    in_maps = []
    for b in range(n_cores):
        d = dict(wm)
        d["xT"] = np.ascontiguousarray(x[b, :S, :].T)
        in_maps.append(d)
    res = run_bass_kernel_spmd(nc, in_maps, core_ids=list(range(n_cores)), trace=trace)
    out = np.stack([np.ascontiguousarray(r["outT"].T) for r in res.results], axis=0)
    return out, res


def kernel(**inputs):
    out, _ = run(inputs)
    return out.astype(np.float32)
```

```python
import math
import numpy as np
import concourse.bass as bass
import concourse.mybir as mybir
from concourse.bass_utils import run_bass_kernel_spmd

F32 = mybir.dt.float32
BF16 = mybir.dt.bfloat16
AF = mybir.ActivationFunctionType
ALU = mybir.AluOpType

D = 1024
DFF = 2816
NFC = 22
NH = 8
TT = 512
ALPHA = (2.0 * 2) ** 0.25
LN_EPS_P = 1e-5 / (ALPHA * ALPHA)
RMS_EPS = 1e-5
LAMBDA_INIT = 0.8 - 0.6 * math.exp(-0.3 * 1)
ATTN_SCALE = 64 ** -0.5
POOL_W = (2, 4, 8, 16)
NS = 6
KG = 4096
WIN_H = tuple(int(160 * 2 ** (h + 1)) for h in range(8))
GH = (128, 256, 256, 512, 512, 512, 512, 512)

ENGS = ("pe", "act", "dve", "pool", "sp")
DEBUG_STAGE = 6
DIAG_SBUF = True


class Buf:
    __slots__ = ("name", "w", "r", "const")

    def __init__(self, name, const=False):
        self.name = name
        self.w = None
        self.r = {}
        self.const = const


class DSem:
    def __init__(self, name):
        self.name = name
        self.count = 0
        self.h = None


class Prog:
    def __init__(self):
        self.ops = {e: [] for e in ENGS}
        self.plan = True
        self.dsems = []

    def dsem(self, name):
        s = DSem(name)
        self.dsems.append(s)
        return s

    def op(self, eng, emit, reads=(), writes=(), dsem=None):
        if self.plan:
            return
        deps = {}

        def add(tok):
            if tok is None:
                return
            k, v = tok
            if deps.get(k, -1) < v:
                deps[k] = v

        for b in reads:
            add(b.w)
        for b in writes:
            add(b.w)
            for k, v in b.r.items():
                add((k, v))
        idx = len(self.ops[eng])
        if dsem is not None:
            dsem.count += 16
            tok = (("d", dsem), dsem.count)
        else:
            tok = (("e", eng), idx)
        for b in writes:
            b.w = tok
            b.r = {}
        for b in reads:
            if b.const or b in writes:
                continue
            k, v = tok
            if b.r.get(k, -1) < v:
                b.r[k] = v
        self.ops[eng].append([emit, deps, dsem, False])

    def emit_all(self, nc):
        for e in ENGS:
            for o in self.ops[e]:
                for (kind, who), v in o[1].items():
                    if kind == "e" and not (who == "pe" and e == "pe"):
                        self.ops[who][v][3] = True
        cum = {}
        for e in ENGS:
            c = 0
            arr = []
            for o in self.ops[e]:
                if o[3] and o[2] is None:
                    c += 1
                arr.append(c)
            cum[e] = arr
        return cum


def _build(S, plan_only=False):
    NT = S // TT
    NB = S // 128
    nc = bass.Bass("TRN2", target_bir_lowering=False)

    def din(name, shape, dt=F32):
        return nc.dram_tensor(name, list(shape), dt, kind="ExternalInput").ap()

    def dscr(name, shape, dt=BF16):
        return nc.dram_tensor(name, list(shape), dt).ap()

    xT = din("xT", [D, S])
    outT = nc.dram_tensor("outT", [D, S], F32, kind="ExternalOutput").ap()
    ffn_names = ["l0f1", "l0f2", "l1f1", "l1f2"]
    w_in_f = {n: din(n + "_win", [NFC * 128, 2048]) for n in ffn_names}
    w_out_f = {n: din(n + "_wout", [8 * 128, DFF]) for n in ffn_names}
    w_q_f = din("wq", [2 * 128, 4096])
    w_k_f = din("wk", [2 * 128, 4096])
    w_v_f = din("wv", [2 * 128, 4096])
    w_o_f = din("wo", [2 * 128, 4096])
    w_p_f = din("wp", [128, 2048])
    vecs_d = din("vecs", [128, 13 * 8])
    subg_d = din("subg", [128, 1])
    lam_d = din("lam", [1, 256])

    w_in_b = {n: dscr(n + "_win_b", [NFC * 128, 2048]) for n in ffn_names}
    w_out_b = {n: dscr(n + "_wout_b", [8 * 128, DFF]) for n in ffn_names}
    w_q_b = dscr("wq_b", [2 * 128, 4096])
    w_k_b = dscr("wk_b", [2 * 128, 4096])
    w_v_b = dscr("wv_b", [2 * 128, 4096])
    w_o_b = dscr("wo_b", [2 * 128, 4096])
    w_p_b = dscr("wp_b", [128, 2048])
    if DEBUG_STAGE >= 50:
        ktc = nc.dram_tensor("ktc", [NH, 128, S], BF16, kind="ExternalOutput").ap()
        vc = nc.dram_tensor("vc", [NH, 128, NB, 128], BF16, kind="ExternalOutput").ap()
    else:
        ktc = dscr("ktc", [NH, 128, S])
        vc = dscr("vc", [NH, 128, NB, 128])

    P = Prog()
    dbg_d = nc.dram_tensor("dbg", [128, 8, TT], F32, kind="ExternalOutput").ap() if DEBUG_STAGE >= 50 else None

    sb = {}

    def salloc(name, shape, dt):
        t = nc.alloc_sbuf_tensor(name, list(shape), dt)
        sb[name] = t
        return t

    XS = [salloc("X%d" % i, [128, 8, 16 + TT], F32) for i in range(2)]
    Z = salloc("Z", [128, 8, TT], F32)
    XBFS = [salloc("XBF%d" % i, [128, 8, TT], BF16) for i in range(2)]
    ATMP = salloc("ATMP", [128, 3, TT], F32)
    H = salloc("H", [128, NFC, TT], BF16)
    ZB = salloc("ZB", [128, 8, TT], BF16)
    ZSQ = salloc("ZSQ", [128, 8, TT], BF16)
    TA = salloc("TA", [128, 2, 16 + TT], F32)
    TB = salloc("TB", [128, 2, 16 + TT], F32)
    SG = salloc("SG", [128, 2, TT], F32)
    LNT = salloc("LNT", [128, 4, TT], F32)
    QT = salloc("QT", [128, 8, TT], BF16)
    PT = salloc("PT", [128, 4, TT], BF16)
    ON = salloc("ON", [128, 8, TT], BF16)
    OSQ = salloc("OSQ", [128, TT], BF16)
    RING = salloc("RING", [128, NS, 4096], BF16)
    ONES = salloc("ONES", [128, 128], BF16)
    TRI = salloc("TRI", [128, 128], BF16)
    VECS = salloc("VECS", [128, 13, 8], F32)
    PSA = salloc("PSA", [128, 8], F32)
    SUBG = salloc("SUBG", [128, 1], F32)
    LAMT = salloc("LAMT", [128, 4, 64], F32)
    LAMS = salloc("LAMS", [128, 8], F32)
    BTG = salloc("BTG", [128, NH, NB + 3], F32)
    IOT = salloc("IOT", [128, NB + 3], F32)
    RC = salloc("RC", [128, 4, 16], F32)
    RCT = salloc("RCT", [128, 16], F32)
    DBG = salloc("DBG", [128, 8, TT], F32) if DEBUG_STAGE >= 50 else None
    bDBG = Buf("DBG")
    PSB = [nc.alloc_psum_tensor("ps%d" % i, [128, TT], F32) for i in range(8)]

    bXS = [[Buf("X%d_%d" % (i, c)) for c in range(8)] for i in range(2)]
    bZ = [Buf("Z%d" % c) for c in range(8)]
    bXBFS = [[Buf("XBF%d_%d" % (i, c)) for c in range(8)] for i in range(2)]
    bATMP = [Buf("ATMP%d" % c) for c in range(3)]

    class Ctx:
        pass
    CTX = []
    for i in range(2):
        C = Ctx()
        C.i, C.X, C.XBF, C.bX, C.bXBF = i, XS[i], XBFS[i], bXS[i], bXBFS[i]
        CTX.append(C)
    bH = [Buf("H%d" % c) for c in range(NFC)]
    bZB = [Buf("ZB%d" % c) for c in range(8)]
    bZSQ = [Buf("ZSQ%d" % c) for c in range(8)]
    bTA = [Buf("TA%d" % c) for c in range(2)]
    bTB = [Buf("TB%d" % c) for c in range(2)]
    bSG = [Buf("SG%d" % c) for c in range(2)]
    bLNT = [Buf("LNT%d" % c) for c in range(4)]
    bQT = [Buf("QT%d" % c) for c in range(8)]
    bPT = [Buf("PT%d" % c) for c in range(4)]
    bON = [Buf("ON%d" % c) for c in range(8)]
    bOSQ = Buf("OSQ")
    bRING = [Buf("RING%d" % c) for c in range(NS)]
    bPS = [Buf("PS%d" % c) for c in range(8)]
    bCONST = Buf("CONST", const=True)
    bRCT = Buf("RCT")
    bKTC = Buf("KTC")
    bVC = Buf("VC")
    bOUT = Buf("OUT")
    bOUTS = [Buf("OUT0"), Buf("OUT1")]
    bW = {n: Buf("W_" + n) for n in
          [f + sfx for f in ("l0f1", "l0f2", "l1f1", "l1f2") for sfx in ("_win", "_wout")] + ["wq", "wk", "wv", "wo", "wp"]}
    sem_ring = [P.dsem("ring%d" % i) for i in range(NS)]
    sem_xs = [P.dsem("xload%d" % i) for i in range(2)]
    sem_outs = [P.dsem("out%d" % i) for i in range(2)]
    sem_out = sem_outs[0]
    sem_kw = P.dsem("kw")
    sem_vw = P.dsem("vw")
    sem_const = P.dsem("const")
    sem_cast = {}
    bWp = {}
    bCHAINS = [Buf("CASTCHAIN0"), Buf("CASTCHAIN1")]
    chain_ctr = [0]

    def next_chain():
        chain_ctr[0] += 1
        return bCHAINS[chain_ctr[0] % 2]

    def cast(name, dst, src):
        b = bW[name]
        s = P.dsem("cast_" + name)
        sem_cast[name] = s
        P.op("pool", lambda e, dst=dst, src=src: e.dma_start(out=dst, in_=src), writes=[b, next_chain()], dsem=s)

    def castp(name, dst, src, k):
        b = Buf("W_%s_%d" % (name, k))
        bWp[(name, k)] = b
        sm = P.dsem("castp_%s_%d" % (name, k))
        P.op("pool", lambda e, dst=dst, src=src: e.dma_start(out=dst, in_=src), writes=[b, next_chain()], dsem=sm)

    def prologue():
        P.op("sp", lambda e: e.dma_start(out=VECS[:], in_=vecs_d.rearrange("p (k c) -> p k c", k=13)),
             writes=[bCONST], dsem=sem_const)
        P.op("sp", lambda e: e.dma_start(out=SUBG[:], in_=subg_d), writes=[bCONST], dsem=sem_const)
        P.op("sp", lambda e: e.dma_start(out=LAMT[:].rearrange("p a b -> p (a b)"),
                                        in_=lam_d.to_broadcast([128, 256])),
             writes=[bCONST], dsem=sem_const)
        P.op("dve", lambda e: e.memset(ONES[:], 1.0), writes=[bCONST])
        for i in range(2):
            P.op("dve", lambda e, i=i: e.memset(XS[i][:].rearrange("p a b -> p (a b)"), 0.0), writes=bXS[i])
        P.op("pool", lambda e: e.memset(TRI[:], 1.0), writes=[bCONST])
        P.op("pool", lambda e: e.affine_select(out=TRI[:], in_=TRI[:], pattern=[[1, 128]],
                                               compare_op=ALU.is_ge, fill=0.0, base=0, channel_multiplier=-1),
             reads=[bCONST], writes=[bCONST])
        P.op("pool", lambda e: e.iota(IOT[:], [[-128, NB + 3]], base=384, channel_multiplier=1,
                                      allow_small_or_imprecise_dtypes=True), writes=[bCONST])
        for h in range(NH):
            slope = 2.0 ** (-(h + 1))
            P.op("pool", lambda e, h=h, slope=slope: e.tensor_scalar(BTG[:, h, :], IOT[:], float(1 - GH[h]), slope,
                                                                      ALU.add, ALU.mult),
                 reads=[bCONST], writes=[bCONST])
        for wi, w in enumerate(POOL_W):
            P.op("dve", lambda e, wi=wi, w=w: e.memset(RC[:, wi, :], 1.0 / w), writes=[bCONST])
            for t in range(w - 1):
                P.op("dve", lambda e, wi=wi, t=t: e.memset(RC[:, wi, t:t + 1], 1.0 / (t + 1)), writes=[bCONST])
        P.op("dve", lambda e: e.tensor_scalar(PSA[:], VECS[:, 12, :], 1.0 / ALPHA, None, ALU.mult),
             reads=[bCONST], writes=[bCONST])
        P.op("dve", lambda e: e.tensor_scalar(SUBG[:], SUBG[:], 1.0 - LAMBDA_INIT, None, ALU.mult),
             reads=[bCONST], writes=[bCONST])
        P.op("dve", lambda e: e.tensor_tensor(LAMT[:, 0, :], LAMT[:, 0, :], LAMT[:, 1, :], ALU.mult),
             reads=[bCONST], writes=[bCONST])
        P.op("dve", lambda e: e.tensor_tensor(LAMT[:, 2, :], LAMT[:, 2, :], LAMT[:, 3, :], ALU.mult),
             reads=[bCONST], writes=[bCONST])
        P.op("dve", lambda e: e.reduce_sum(LAMS[:, 0:1], LAMT[:, 0, :], axis=mybir.AxisListType.X),
             reads=[bCONST], writes=[bCONST])
        P.op("dve", lambda e: e.reduce_sum(LAMS[:, 1:2], LAMT[:, 2, :], axis=mybir.AxisListType.X),
             reads=[bCONST], writes=[bCONST])
        P.op("act", lambda e: e.activation(LAMS[:, 2:4], LAMS[:, 0:2], AF.Exp), reads=[bCONST], writes=[bCONST])
        P.op("dve", lambda e: e.tensor_tensor(LAMS[:, 4:5], LAMS[:, 2:3], LAMS[:, 3:4], ALU.subtract),
             reads=[bCONST], writes=[bCONST])
        P.op("dve", lambda e: e.tensor_scalar(LAMS[:, 5:6], LAMS[:, 4:5], LAMBDA_INIT, -1.0, ALU.add, ALU.mult),
             reads=[bCONST], writes=[bCONST])
        order = []
        for n in ffn_names[:2]:
            order.append((n + "_win", w_in_b[n], w_in_f[n]))
            order.append((n + "_wout", w_out_b[n], w_out_f[n]))
            if n == "l0f1":
                order.append(("wp", w_p_b, w_p_f))
        n = "l1f1"
        order.append((n + "_win", w_in_b[n], w_in_f[n]))
        order.append((n + "_wout", w_out_b[n], w_out_f[n]))
        order += [("wq", w_q_b, w_q_f), ("wk", w_k_b, w_k_f), ("wv", w_v_b, w_v_f), ("wo", w_o_b, w_o_f)]
        n = "l1f2"
        order.append((n + "_win", w_in_b[n], w_in_f[n]))
        order.append((n + "_wout", w_out_b[n], w_out_f[n]))
        for name, dst, src in order:
            if name.endswith("_win"):
                for jj in range(NFC // 2):
                    castp(name, dst[jj * 256:(jj + 1) * 256, :], src[jj * 256:(jj + 1) * 256, :], jj)
            elif name.endswith("_wout"):
                for c in range(8):
                    castp(name, dst[c * 128:(c + 1) * 128, :], src[c * 128:(c + 1) * 128, :], c)
            else:
                cast(name, dst, src)

    plan = []
    state = {"issued": 0, "consumed": 0}
    fired = set()
    closed = set()

    def ring_issue():
        while state["issued"] < len(plan):
            i = state["issued"]
            if i >= NS and (i - NS) not in closed:
                break
            mk, reads, ev = plan[i]
            if ev is not None and ev not in fired:
                break
            if callable(reads):
                reads = reads()
            s = i % NS
            P.op("sp", mk(s), reads=reads, writes=[bRING[s]], dsem=sem_ring[s])
            state["issued"] += 1

    def ring_next(mk, reads, ev=None):
        if P.plan:
            plan.append((mk, reads, ev))
            return len(plan) - 1, 0
        i = state["consumed"]
        state["consumed"] += 1
        ring_issue()
        assert state["issued"] > i, "ring item %d not issued (event %s)" % (i, plan[i][2])
        return i, i % NS

    def ring_close(i):
        if P.plan:
            return
        closed.add(i)
        ring_issue()

    def ld_rows(src, r0, ncols):
        def mk(s):
            return lambda e: e.dma_start(out=RING[:, s, 0:ncols], in_=src[r0:r0 + 128, 0:ncols])
        return mk

    def mm(out, lhsT, rhs, start, stop, reads, writes):
        P.op("pe", lambda e: e.matmul(out, lhsT, rhs, start=start, stop=stop), reads=reads, writes=writes)

    TQ = []

    def tick(n=1):
        for _ in range(n):
            if TQ:
                TQ.pop(0)()

    def flush():
        tick(len(TQ))

    def ln_steps(C, gk, bk):
        X, XBF, bX, bXBF = C.X, C.XBF, C.bX, C.bXBF
        MEAN, MSQ, VAR, RSTD = (LNT[:, i, :] for i in range(4))
        LNV = MSQ
        steps = []

        def sq(c0):
            def f():
                for c in (c0, c0 + 1):
                    P.op("act", lambda e, c=c: e.activation(ZSQ[:, c, :], Z[:, c, :], AF.Square),
                         reads=[bZ[c]], writes=[bZSQ[c]])
                    P.op("act", lambda e, c=c: e.activation(ZB[:, c, :], Z[:, c, :], AF.Copy), reads=[bZ[c]], writes=[bZB[c]])
            return f
        for c0 in (0, 2, 4, 6):
            steps.append(sq(c0))

        def st1():
            for c in range(8):
                mm(PSB[6][:], ONES[:], ZB[:, c, :], c == 0, c == 7, [bZB[c], bCONST], [bPS[6]])

        def st2():
            for c in range(8):
                mm(PSB[7][:], ONES[:], ZSQ[:, c, :], c == 0, c == 7, [bZSQ[c], bCONST], [bPS[7]])

        def st3():
            P.op("dve", lambda e: e.tensor_scalar(MEAN, PSB[6][:], 1.0 / D, None, ALU.mult),
                 reads=[bPS[6]], writes=[bLNT[0]])
            P.op("dve", lambda e: e.tensor_tensor(MSQ, MEAN, MEAN, ALU.mult), reads=[bLNT[0]], writes=[bLNT[1]])
            P.op("dve", lambda e: e.scalar_tensor_tensor(VAR, PSB[7][:], 1.0 / D, MSQ, ALU.mult, ALU.subtract),
                 reads=[bPS[7], bLNT[1]], writes=[bLNT[2]])
            P.op("act", lambda e: e.activation(LNV, VAR, AF.Ln, bias=EPSC[:, 0:1]), reads=[bLNT[2], bCONST], writes=[bLNT[1]])
            P.op("act", lambda e: e.activation(RSTD, LNV, AF.Exp, scale=-0.5), reads=[bLNT[1]], writes=[bLNT[3]])
        steps += [st1, st2, st3]

        def nz(c):
            def f():
                P.op("dve", lambda e: e.tensor_tensor(Z[:, c, :], Z[:, c, :], MEAN, ALU.subtract),
                     reads=[bZ[c], bLNT[0]], writes=[bZ[c]])
                P.op("dve", lambda e: e.tensor_tensor(Z[:, c, :], Z[:, c, :], RSTD, ALU.mult),
                     reads=[bZ[c], bLNT[3]], writes=[bZ[c]])
                P.op("dve", lambda e: e.tensor_scalar(XBF[:, c, :], Z[:, c, :], VECS[:, gk, c:c + 1],
                                                      VECS[:, bk, c:c + 1], ALU.mult, ALU.add),
                     reads=[bZ[c], bCONST], writes=[bXBF[c]])
                P.op("act", lambda e: e.activation(X[:, c, 16:], Z[:, c, :], AF.Identity,
                                                   bias=VECS[:, bk, c:c + 1], scale=VECS[:, gk, c:c + 1]),
                     reads=[bZ[c], bCONST], writes=[bX[c]])
            return f
        for c in range(8):
            steps.append(nz(c))
        return steps

    def ffn(C, name):
        X, XBF, bX, bXBF = C.X, C.XBF, C.bX, C.bXBF
        wi, wo = w_in_b[name], w_out_b[name]
        bwi, bwo = bW[name + "_win"], bW[name + "_wout"]
        for jj in range(NFC // 2):
            def mk(s, jj=jj):
                return lambda e: e.dma_start(
                    out=RING[:, s, :].rearrange("p (u f) -> p u f", u=2),
                    in_=wi[jj * 256:(jj + 1) * 256, :].rearrange("(u p) f -> p u f", u=2))
            it, s = ring_next(mk, lambda jj=jj: [bWp.get((name + "_win", jj), bwi)])
            for u in range(2):
                j = 2 * jj + u
                G, U = PSB[j % 2], PSB[2 + j % 2]
                base = u * 2048
                for c in range(8):
                    mm(G[:], RING[:, s, base + c * 256: base + c * 256 + 128], XBF[:, c, :], c == 0, c == 7,
                       [bRING[s], bXBF[c]], [bPS[j % 2]])
                for c in range(8):
                    mm(U[:], RING[:, s, base + c * 256 + 128: base + c * 256 + 256], XBF[:, c, :], c == 0, c == 7,
                       [bRING[s], bXBF[c]], [bPS[2 + j % 2]])
                P.op("act", lambda e, j=j, G=G: e.activation(SG[:, j % 2, :], G[:], AF.Silu),
                     reads=[bPS[j % 2]], writes=[bSG[j % 2]])
                P.op("dve", lambda e, j=j, U=U: e.tensor_tensor(H[:, j, :], SG[:, j % 2, :], U[:], ALU.mult),
                     reads=[bSG[j % 2], bPS[2 + j % 2]], writes=[bH[j]])
                tick(2)
            ring_close(it)
        flush()
        for c in range(8):
            it, s = ring_next(ld_rows(wo, c * 128, DFF), lambda c=c: [bWp.get((name + "_wout", c), bwo)])
            Y = PSB[4 + c % 2]
            for j in range(NFC):
                mm(Y[:], RING[:, s, j * 128:(j + 1) * 128], H[:, j, :], j == 0, j == NFC - 1,
                   [bRING[s], bH[j]], [bPS[4 + c % 2]])
            ring_close(it)
            P.op("dve", lambda e, c=c, Y=Y: e.scalar_tensor_tensor(Z[:, c, :], Y[:], 0.5 / ALPHA, X[:, c, 16:],
                                                                   ALU.mult, ALU.add),
                 reads=[bPS[4 + c % 2], bX[c]], writes=[bZ[c]])
            tick()

    def pool_steps(C, Cn, first_tile):
        X, XBF, bX, bXBF = C.X, C.XBF, C.bX, C.bXBF
        DD, bDD = ZB, bZB
        steps = []
        st = {}

        def chunk(c):
            def f():
                w = POOL_W[c // 2]
                k = c % 2
                eng = "pool" if c % 2 == 0 else "dve"
                nlev = int(math.log2(w))
                cur = None
                for lev in range(1, nlev + 1):
                    sh = 1 << (lev - 1)
                    lo = 16 - (w - (1 << lev))
                    dstT, bdst = (TA, bTA) if lev % 2 == 1 else (TB, bTB)
                    if lev == 1:
                        in0, in1 = X[:, c, lo:], X[:, c, lo - sh:16 + TT - sh]
                        rd = [bX[c]]
                    else:
                        pT, bp = (TB, bTB) if lev % 2 == 1 else (TA, bTA)
                        in0, in1 = pT[:, k, lo:], pT[:, k, lo - sh:16 + TT - sh]
                        rd = [bp[k]]
                    P.op(eng, lambda e, dstT=dstT, k=k, lo=lo, in0=in0, in1=in1: e.tensor_tensor(
                        dstT[:, k, lo:], in0, in1, ALU.add), reads=rd, writes=[bdst[k]])
                    cur, bcur = dstT, bdst
                P.op("dve", lambda e, c=c, cur=cur, k=k, w=w: e.scalar_tensor_tensor(
                    DD[:, c, :], cur[:, k, 16:], 1.0 / w, X[:, c, 16:], ALU.mult, ALU.subtract),
                    reads=[bcur[k], bX[c]], writes=[bDD[c]])
                if first_tile:
                    wi = c // 2
                    P.op("dve", lambda e, cur=cur, k=k, wi=wi: e.tensor_tensor(RCT[:], cur[:, k, 16:32], RC[:, wi, :], ALU.mult),
                         reads=[bcur[k], bCONST], writes=[bRCT])
                    P.op("dve", lambda e, c=c: e.tensor_tensor(DD[:, c, 0:16], RCT[:], X[:, c, 16:32], ALU.subtract),
                         reads=[bRCT, bX[c]], writes=[bDD[c]])
            return f
        for c in range(8):
            steps.append(chunk(c))

        def halo():
            for c in range(8):
                P.op("pool", lambda e, c=c: e.tensor_copy(Cn.X[:, c, 0:16], X[:, c, TT:TT + 16]), reads=[bX[c]], writes=[Cn.bX[c]])
        steps.append(halo)

        def grp(g):
            def f():
                if g == 0:
                    st["it"], st["s"] = ring_next(ld_rows(w_p_b, 0, 2048), [bW["wp"]])
                s_ = st["s"]
                for oc in range(2):
                    c = 2 * g + oc
                    Y = PSB[4 + c % 2]
                    for ic in range(2):
                        col = ((g * 2 + oc) * 2 + ic) * 128
                        mm(Y[:], RING[:, s_, col:col + 128], DD[:, 2 * g + ic, :], ic == 0, ic == 1,
                           [bRING[s_], bDD[2 * g + ic]], [bPS[4 + c % 2]])
                    P.op("dve", lambda e, c=c, Y=Y: e.scalar_tensor_tensor(Z[:, c, :], Y[:], PSA[:, c:c + 1], X[:, c, 16:],
                                                                           ALU.mult, ALU.add),
                         reads=[bPS[4 + c % 2], bX[c], bCONST], writes=[bZ[c]])
                if g == 3:
                    ring_close(st["it"])
            return f
        for g in range(4):
            steps.append(grp(g))
        return steps

    def qkv(C, j):
        X, XBF, bX, bXBF = C.X, C.XBF, C.bX, C.bXBF
        t0 = j * TT
        KTt, Vt = H[:, 0:8, :], H[:, 8:16, :]
        bZB, bZSQ = bH[0:8], bH[8:16]
        for which, wsrc, bname in (("q", w_q_b, "wq"), ("k", w_k_b, "wk")):
            for grp in range(2):
                it, s = ring_next(ld_rows(wsrc, grp * 128, 4096), [bW[bname]])
                for hh in range(4):
                    h = grp * 4 + hh
                    pb = h % 4
                    for c in range(8):
                        col = (hh * 8 + c) * 128
                        mm(PSB[pb][:], RING[:, s, col:col + 128], XBF[:, c, :], c == 0, c == 7,
                           [bRING[s], bXBF[c]], [bPS[pb]])
                    if which == "q":
                        P.op("act", lambda e, h=h, pb=pb: e.activation(QT[:, h, :], PSB[pb][:], AF.Copy),
                             reads=[bPS[pb]], writes=[bQT[h]])
                    else:
                        P.op("dve", lambda e, h=h, pb=pb: e.tensor_copy(KTt[:, h, :], PSB[pb][:]),
                             reads=[bPS[pb]], writes=[bZB[h]])
                    tick()
                ring_close(it)
        P.op("sp", lambda e: e.dma_start(out=ktc.rearrange("h p t -> p h t")[:, :, t0:t0 + TT], in_=KTt),
             reads=bZB, writes=[bKTC], dsem=sem_kw)
        for half in range(2):
            it, s = ring_next(ld_rows(w_v_b, half * 128, 4096), [bW["wv"]])
            for sub in range(4):
                pb = (half * 4 + sub) % 4
                for c in range(8):
                    mm(PSB[pb][:], XBF[:, c, sub * 128:(sub + 1) * 128], RING[:, s, c * 512:(c + 1) * 512],
                       c == 0, c == 7, [bRING[s], bXBF[c]], [bPS[pb]])
                dst = Vt.rearrange("p (s a) t -> p s (a t)", s=4)[:, sub, half * 512:(half + 1) * 512]
                if sub % 2 == 0:
                    P.op("act", lambda e, dst=dst, pb=pb: e.activation(dst, PSB[pb][:], AF.Copy),
                         reads=[bPS[pb]], writes=[bZSQ[2 * sub], bZSQ[2 * sub + 1]])
                else:
                    P.op("dve", lambda e, dst=dst, pb=pb: e.tensor_copy(dst, PSB[pb][:]),
                         reads=[bPS[pb]], writes=[bZSQ[2 * sub], bZSQ[2 * sub + 1]])
                tick()
            ring_close(it)
        Vv = Vt.rearrange("p (s a) t -> p s (a t)", s=4)
        for h in range(NH):
            P.op("sp", lambda e, h=h: e.dma_start(out=vc[h, :, 4 * j:4 * j + 4, :], in_=Vv[:, :, h * 128:(h + 1) * 128]),
                 reads=bZSQ, writes=[bVC], dsem=sem_vw)
        fired.add(("kv", j))

    def attention(j):
        nkb = 4 * (j + 1)
        KB = KG // 128
        deferred = []

        def run_deferred(n):
            for _ in range(n):
                if deferred:
                    deferred.pop(0)()

        for h in range(NH):
            G = GH[h]
            gs = G // 128
            O1, O2 = PSB[4], PSB[5]
            a = h % 2
            ginfo = {}
            bmin = max(0, (j * TT - WIN_H[h]) // 128)

            def get_group(g, h=h, ginfo=ginfo, bmin=bmin):
                if g not in ginfo:
                    b0 = bmin + g * KB
                    b1 = min(4 * j if DIAG_SBUF else nkb, b0 + KB)
                    nk = (b1 - b0) * 128

                    def mkk(s, h=h, b0=b0, nk=nk):
                        return lambda e: e.dma_start(out=RING[:, s, 0:nk], in_=ktc[h, :, b0 * 128:b0 * 128 + nk])

                    def mkv(s, h=h, b0=b0, b1=b1, nk=nk):
                        return lambda e: e.dma_start(out=RING[:, s, 0:nk].rearrange("p (b e) -> p b e", e=128),
                                                     in_=vc[h, :, b0:b1, :])
                    ki, sk = ring_next(mkk, [bKTC], ("kv", j - 1 if DIAG_SBUF else j))
                    vi, sv = ring_next(mkv, [bVC], ("kv", j - 1 if DIAG_SBUF else j))
                    ginfo[g] = dict(ki=ki, sk=sk, vi=vi, sv=sv, b0=b0, b1=b1)
                return ginfo[g]

            def s_stage(b, h=h, G=G, gs=gs, bmin=bmin):
                c0 = max(0, b - 4 * j)
                q0 = 128 * c0
                i2 = b % 2
                if DIAG_SBUF and b >= 4 * j:
                    k1 = H[0:64, h, c0 * 128:(c0 + 1) * 128]
                    k2 = H[64:128, h, c0 * 128:(c0 + 1) * 128]
                    kb_ = bH[h]
                    gi = None
                else:
                    gi = get_group((b - bmin) // KB)
                    bl = b - gi["b0"]
                    sk = gi["sk"]
                    k1 = RING[0:64, sk, bl * 128:(bl + 1) * 128]
                    k2 = RING[64:128, sk, bl * 128:(bl + 1) * 128]
                    kb_ = bRING[sk]
                mm(PSB[i2][:, q0:], k1, QT[0:64, h, q0:], True, True, [kb_, bQT[h]], [bPS[i2]])
                mm(PSB[2 + i2][:, q0:], k2, QT[64:128, h, q0:], True, True, [kb_, bQT[h]], [bPS[2 + i2]])
                if gi is not None and b == gi["b1"] - 1:
                    ring_close(gi["ki"])
                for mp in range(2):
                    for gq in range(TT // G):
                        lo = max(q0, gq * G)
                        hi = (gq + 1) * G
                        if lo >= hi:
                            continue
                        n = 4 * j + gq * gs - b
                        P.op("act", lambda e, mp=mp, i2=i2, lo=lo, hi=hi, n=n, h=h: e.activation(
                            PT[:, 2 * mp + i2, lo:hi], PSB[2 * mp + i2][:, lo:hi], AF.Exp,
                            bias=BTG[:, h, n + 3:n + 4], scale=ATTN_SCALE),
                            reads=[bPS[2 * mp + i2], bCONST], writes=[bPT[2 * mp + i2]])
                if b >= 4 * j:
                    for mp in range(2):
                        P.op("dve", lambda e, mp=mp, i2=i2, q0=q0: e.tensor_tensor(
                            PT[:, 2 * mp + i2, q0:q0 + 128], PT[:, 2 * mp + i2, q0:q0 + 128], TRI[:], ALU.mult),
                            reads=[bPT[2 * mp + i2], bCONST], writes=[bPT[2 * mp + i2]])

            def o_stage(b, h=h, a=a, O1=O1, O2=O2, bmin=bmin):
                c0 = max(0, b - 4 * j)
                q0 = 128 * c0
                i2 = b % 2
                first, last = (b == bmin), (b == nkb - 1)
                if DIAG_SBUF and b >= 4 * j:
                    vblk = H[:, 8:16, :].rearrange("p (s a) t -> p s (a t)", s=4)[:, c0, h * 128:(h + 1) * 128]
                    vb_ = [bH[8 + 2 * c0], bH[8 + 2 * c0 + 1]]
                    gi = None
                else:
                    gi = get_group((b - bmin) // KB)
                    bl = b - gi["b0"]
                    sv = gi["sv"]
                    vblk = RING[:, sv, bl * 128:(bl + 1) * 128]
                    vb_ = [bRING[sv]]
                for mp, Oa in enumerate((O1, O2)):
                    mm(Oa[:, q0:], vblk, PT[:, 2 * mp + i2, q0:], first, last,
                       vb_ + [bPT[2 * mp + i2]], [bPS[4 + mp]])
                    mm(PSB[6 + mp][:, q0:], ONES[:], PT[:, 2 * mp + i2, q0:], first, last,
                       [bCONST, bPT[2 * mp + i2]], [bPS[6 + mp]])
                if gi is not None and b == gi["b1"] - 1:
                    ring_close(gi["vi"])
            prev = None
            for b in range(bmin, nkb):
                s_stage(b)
                run_deferred(1)
                tick()
                if prev is not None:
                    o_stage(prev)
                prev = b
            o_stage(prev)
            run_deferred(len(deferred))

            R1, R2, T1, T2 = SG[:, 0, :], SG[:, 1, :], TA[:, 0, 0:TT], TA[:, 1, 0:TT]
            OO, OS1, OS2, SS1, SS2 = TB[:, 0, 0:TT], TB[:, 1, 0:TT], ATMP[:, 0, :], ATMP[:, 1, :], ATMP[:, 2, :]
            bZ = [bSG[0], bSG[1], bTA[0], bTA[1], bTB[0], bTB[1], bATMP[0], bATMP[1]]
            bACC = [bATMP[2]]
            P.op("dve", lambda e: e.tensor_copy(OS1, PSB[4][:]), reads=[bPS[4]], writes=[bZ[5]])
            P.op("act", lambda e: e.activation(OS2, PSB[5][:], AF.Copy), reads=[bPS[5]], writes=[bZ[6]])
            P.op("dve", lambda e: e.tensor_copy(SS1, PSB[6][:]), reads=[bPS[6]], writes=[bZ[7]])
            P.op("act", lambda e: e.activation(SS2, PSB[7][:], AF.Copy), reads=[bPS[7]], writes=[bACC[0]])

            def f2():
                P.op("dve", lambda e: e.reciprocal(R1, SS1), reads=[bZ[7]], writes=[bZ[0]])

            def f3():
                P.op("dve", lambda e: e.reciprocal(R2, SS2), reads=[bACC[0]], writes=[bZ[1]])

            def f4():
                P.op("dve", lambda e: e.tensor_tensor(T1, OS1, R1, ALU.mult), reads=[bZ[5], bZ[0]], writes=[bZ[2]])
                P.op("dve", lambda e: e.tensor_tensor(T2, OS2, R2, ALU.mult), reads=[bZ[6], bZ[1]], writes=[bZ[3]])

            def f5():
                P.op("dve", lambda e: e.scalar_tensor_tensor(OO, T2, LAMS[:, 5:6], T1, ALU.mult, ALU.add),
                     reads=[bZ[2], bZ[3], bCONST], writes=[bZ[4]])
                P.op("act", lambda e: e.activation(OSQ[:], OO, AF.Square), reads=[bZ[4]], writes=[bOSQ])

            def f6():
                mm(PSB[3][:], ONES[:], OSQ[:], True, True, [bOSQ, bCONST], [bPS[3]])
                P.op("act", lambda e: e.activation(R1, PSB[3][:], AF.Ln, bias=EPSC[:, 1:2], scale=1.0 / 128),
                     reads=[bPS[3], bCONST], writes=[bZ[0]])

            def f7():
                P.op("act", lambda e: e.activation(R2, R1, AF.Exp, scale=-0.5), reads=[bZ[0]], writes=[bZ[1]])

            def f8(h=h):
                P.op("dve", lambda e: e.tensor_tensor(T1, OO, R2, ALU.mult), reads=[bZ[4], bZ[1]], writes=[bZ[2]])
                P.op("dve", lambda e, h=h: e.tensor_scalar(ON[:, h, :], T1, SUBG[:, 0:1], None, ALU.mult),
                     reads=[bZ[2], bCONST], writes=[bON[h]])

            nop = lambda: None
            deferred.extend([f2, nop, nop, f3, nop, nop, f4, f5, nop, nop, f6, nop, f7, nop, f8])
        run_deferred(len(deferred))

    def wo_proj(C):
        X, XBF, bX, bXBF = C.X, C.XBF, C.bX, C.bXBF
        flush()
        for grp in range(2):
            it, s = ring_next(ld_rows(w_o_b, grp * 128, 4096), [bW["wo"]])
            for cl in range(4):
                c = grp * 4 + cl
                Y = PSB[c % 2]
                for h in range(NH):
                    col = (cl * 8 + h) * 128
                    mm(Y[:], RING[:, s, col:col + 128], ON[:, h, :], h == 0, h == NH - 1,
                       [bRING[s], bON[h]], [bPS[c % 2]])
                P.op("dve", lambda e, c=c, Y=Y: e.scalar_tensor_tensor(Z[:, c, :], Y[:], 1.0 / ALPHA, X[:, c, 16:],
                                                                       ALU.mult, ALU.add),
                     reads=[bPS[c % 2], bX[c]], writes=[bZ[c]])
                tick()
            ring_close(it)

    def load_tile(C, j):
        X, XBF, bX, bXBF = C.X, C.XBF, C.bX, C.bXBF
        t0 = j * TT
        P.op("sp", lambda e: e.dma_start(out=X[:, :, 16:],
                                        in_=xT.rearrange("(c p) t -> p c t", p=128)[:, :, t0:t0 + TT]),
             writes=bX, dsem=sem_xs[C.i])
        for c in range(8):
            if c % 2 == 0:
                P.op("act", lambda e, c=c: e.activation(XBF[:, c, :], X[:, c, 16:], AF.Copy), reads=[bX[c]], writes=[bXBF[c]])
            else:
                P.op("dve", lambda e, c=c: e.tensor_copy(XBF[:, c, :], X[:, c, 16:]), reads=[bX[c]], writes=[bXBF[c]])

    def store_tile(C, j):
        t0 = j * TT
        P.op("sp", lambda e: e.dma_start(out=outT.rearrange("(c p) t -> p c t", p=128)[:, :, t0:t0 + TT],
                                        in_=C.X[:, :, 16:]),
             reads=C.bX, writes=[bOUTS[C.i]], dsem=sem_outs[C.i])

    def attn_stage(C, j):
        qkv(C, j)
        attention(j)
        wo_proj(C)

    STAGES = [
        (lambda C, j: ffn(C, "l0f1"),
         lambda C, j: ln_steps(C, 0, 1) + pool_steps(C, CTX[1 - C.i], j == 0) + ln_steps(C, 2, 3)),
        (lambda C, j: ffn(C, "l0f2"), lambda C, j: ln_steps(C, 4, 5)),
        (lambda C, j: ffn(C, "l1f1"), lambda C, j: ln_steps(C, 6, 7)),
        (attn_stage, lambda C, j: ln_steps(C, 8, 9)),
        (lambda C, j: ffn(C, "l1f2"), lambda C, j: ln_steps(C, 10, 11)),
    ]

    def run_tiles():
        CA, CB = CTX
        pend = None
        for jA in range(0, NT, 2):
            jB = jA + 1 if jA + 1 < NT else None
            load_tile(CA, jA)
            for k, (S, TR) in enumerate(STAGES):
                S(CA, jA)
                flush()
                if k == 0:
                    if pend is not None:
                        store_tile(CB, pend)
                        pend = None
                    if jB is not None:
                        load_tile(CB, jB)
                TQ.extend(TR(CA, jA))
                if jB is not None:
                    S(CB, jB)
                    flush()
                    TQ.extend(TR(CB, jB))
                else:
                    flush()
            store_tile(CA, jA)
            pend = jB
        flush()
        if pend is not None:
            store_tile(CB, pend)

    EPSC = salloc("EPSC", [128, 2], F32)

    def program():
        if not P.plan:
            prologue()
            P.op("dve", lambda e: e.memset(EPSC[:, 0:1], LN_EPS_P), writes=[bCONST])
            P.op("dve", lambda e: e.memset(EPSC[:, 1:2], RMS_EPS), writes=[bCONST])
        run_tiles()

    P.plan = True
    program()
    fired.clear()
    P.plan = False
    program()
    assert state["consumed"] == len(plan) and state["issued"] == len(plan)
    cum = P.emit_all(nc)
    if plan_only:
        return nc, P

    import contextlib
    with contextlib.ExitStack() as es:
        esem = {e: es.enter_context(nc.semaphore("e_" + e)) for e in ENGS}
        for ds in P.dsems:
            ds.h = es.enter_context(nc.semaphore("d_" + ds.name))
        fin = es.enter_context(nc.semaphore("fin"))
        block = es.enter_context(nc.Block())

        def replay(ename, eng):
            clock = {}
            for idx, (emit, deps, dsem, marked) in enumerate(P.ops[ename]):
                for (kind, who), v in deps.items():
                    if kind == "e":
                        if who == "pe" and ename == "pe":
                            continue
                        sem, val = esem[who], cum[who][v]
                    else:
                        sem, val = who.h, v
                    key = (kind, who if kind == "e" else who.name)
                    if clock.get(key, 0) >= val:
                        continue
                    clock[key] = val
                    eng.wait_ge(sem, val)
                ins = emit(eng)
                if dsem is not None:
                    ins.then_inc(dsem.h, 16)
                elif marked:
                    ins.then_inc(esem[ename], 1)
            if ename == "sp":
                for so in sem_outs:
                    if so.count:
                        eng.wait_ge(so.h, so.count)

        @block.tensor
        def _(e):
            replay("pe", e)

        @block.scalar
        def _(e):
            replay("act", e)

        @block.vector
        def _(e):
            replay("dve", e)

        @block.gpsimd
        def _(e):
            replay("pool", e)

        @block.sync
        def _(e):
            replay("sp", e)
    return nc, P


def _prep_weights(inp):
    def r_win(w):
        g = w[:, :DFF].reshape(8, 128, NFC, 128)
        u = w[:, DFF:].reshape(8, 128, NFC, 128)
        o = np.stack([g, u], axis=3)
        o = o.transpose(2, 1, 0, 3, 4)
        return np.ascontiguousarray(o).reshape(NFC * 128, 2048)

    def r_wout(w):
        o = w.reshape(NFC, 128, 8, 128).transpose(2, 1, 0, 3)
        return np.ascontiguousarray(o).reshape(8 * 128, DFF)

    m = {}
    for n, pre in (("l0f1", "l0_ffn1"), ("l0f2", "l0_ffn2"), ("l1f1", "l1_ffn1"), ("l1f2", "l1_ffn2")):
        m[n + "_win"] = r_win(np.asarray(inp[pre + "_w_in"], np.float32))
        m[n + "_wout"] = r_wout(np.asarray(inp[pre + "_w_out"], np.float32))
    wqkv = np.asarray(inp["l1_w_qkv"], np.float32)

    def r_qk(off1, off2):
        a = wqkv[:, off1:off1 + 512].reshape(8, 128, 2, 4, 64)
        b = wqkv[:, off2:off2 + 512].reshape(8, 128, 2, 4, 64)
        o = np.concatenate([a, b], axis=4)
        o = o.transpose(2, 1, 3, 0, 4)
        return np.ascontiguousarray(o).reshape(2 * 128, 4096)

    m["wq"] = r_qk(0, 512)
    m["wk"] = r_qk(1024, 1536)
    v = wqkv[:, 2048:3072].reshape(8, 128, 2, 512).transpose(2, 1, 0, 3)
    m["wv"] = np.ascontiguousarray(v).reshape(2 * 128, 4096)
    wo = np.asarray(inp["l1_w_o"], np.float32).reshape(8, 128, 2, 4, 128)
    m["wo"] = np.ascontiguousarray(wo.transpose(2, 1, 3, 0, 4)).reshape(2 * 128, 4096)
    pw = np.asarray(inp["l0_pool_w"], np.float32).reshape(4, 2, 128, 2, 128)
    m["wp"] = np.ascontiguousarray(pw.transpose(2, 0, 3, 1, 4)).reshape(128, 2048)
    names = ["l0_ln1_g", "l0_ln1_b", "l0_ln2_g", "l0_ln2_b", "l0_ln3_g", "l0_ln3_b",
             "l1_ln1_g", "l1_ln1_b", "l1_ln2_g", "l1_ln2_b", "l1_ln3_g", "l1_ln3_b", "l0_pool_scale"]
    vs = np.stack([np.asarray(inp[k], np.float32).reshape(8, 128) for k in names], axis=0)
    m["vecs"] = np.ascontiguousarray(vs.transpose(2, 0, 1)).reshape(128, 13 * 8)
    m["subg"] = np.asarray(inp["l1_subln_g"], np.float32).reshape(128, 1).copy()
    m["lam"] = np.concatenate([np.asarray(inp[k], np.float32).reshape(64) for k in
                               ("l1_lam_q1", "l1_lam_k1", "l1_lam_q2", "l1_lam_k2")]).reshape(1, 256).copy()
    return m


_CACHE = {}


def run(inputs, S=None, n_cores=8, trace=False):
    x = np.asarray(inputs["x"], np.float32)
    B = x.shape[0]
    if S is None:
        S = x.shape[1]
    wm = _prep_weights(inputs)
    key = (S, DEBUG_STAGE)
    if key not in _CACHE:
        _CACHE[key] = _build(S)[0]
    nc = _CACHE[key]
    in_maps = []
    for b in range(n_cores):
        d = dict(wm)
        d["xT"] = np.ascontiguousarray(x[b, :S, :].T)
        in_maps.append(d)
    res = run_bass_kernel_spmd(nc, in_maps, core_ids=list(range(n_cores)), trace=trace)
    out = np.stack([np.ascontiguousarray(r["outT"].T) for r in res.results], axis=0)
    return out, res


def kernel(**inputs):
    out, _ = run(inputs)
    return out.astype(np.float32)
```
